# Optimizing a Trainium2 kernel written in Bass

```python
import math
import jax, jax.numpy as jnp
from jax import lax
import numpy as np

D_MODEL = 1024
BATCH = 32
SEQ = 2048
DEPTH = 2

GRID_W = 64
CTX_LEN = 256
EPS = 1e-6

NA_HEADS = 8
NA_HEAD_DIM = 64
NA_WIDTH = NA_HEADS * NA_HEAD_DIM
NA_WIN_H = 8
NA_WIN_W = 16
NA_COL_BLOCK = 16
NA_KEY_COLS = 2 * NA_WIN_W
NA_SCALE = NA_HEAD_DIM ** -0.5

MLA_HEADS = 8
MLA_NOPE_DIM = 64
MLA_ROPE_DIM = 32
MLA_V_DIM = 64
MLA_WIDTH = MLA_HEADS * MLA_V_DIM
MLA_Q_RANK = 256
MLA_KV_RANK = 128
MLA_Q_BLOCK = 128
MLA_SCALE = (MLA_NOPE_DIM + MLA_ROPE_DIM) ** -0.5
ROPE_THETA = 10000.0

D_MIX = NA_WIDTH + MLA_WIDTH
IN_SPLITS = (NA_WIDTH, NA_WIDTH, NA_WIDTH, NA_WIDTH,
             MLA_Q_RANK, MLA_KV_RANK, MLA_ROPE_DIM, MLA_WIDTH)
D_IN = sum(IN_SPLITS)

kernel_name = "hybrid_natten_mla_dit_prefix"


def rmsnorm(x, g):
    xf = x.astype(jnp.float32)
    y = xf * lax.rsqrt(jnp.mean(xf * xf, axis=-1, keepdims=True) + EPS)
    return (y * g.astype(jnp.float32)).astype(x.dtype)


def axial_rope_tables(n_tokens):
    t = jnp.arange(n_tokens)
    row = (t // GRID_W).astype(jnp.float32)
    col = (t % GRID_W).astype(jnp.float32)
    per_axis = MLA_ROPE_DIM // 2
    inv = 1.0 / (ROPE_THETA ** (jnp.arange(0, per_axis, 2, dtype=jnp.float32) / per_axis))
    ang = jnp.concatenate([row[:, None] * inv[None], col[:, None] * inv[None]], axis=-1)
    return jnp.cos(ang), jnp.sin(ang)


def apply_rope(x, cos, sin):
    xp = x.reshape(x.shape[:-1] + (MLA_ROPE_DIM // 2, 2))
    x1, x2 = xp[..., 0], xp[..., 1]
    bshape = (cos.shape[0],) + (1,) * (x.ndim - 3) + (cos.shape[1],)
    cs = cos.reshape(bshape).astype(x.dtype)
    sn = sin.reshape(bshape).astype(x.dtype)
    return jnp.stack([x1 * cs - x2 * sn, x1 * sn + x2 * cs], axis=-1).reshape(x.shape)


def project(h, w_in, q_norm_g, w_uq, kv_norm_g, w_ukv):
    B, T, _ = h.shape
    cuts = [int(i) for i in np.cumsum(IN_SPLITS)[:-1]]
    na_q, na_k, na_v, na_gate, c_q, c_kv, k_rope, mla_gate = jnp.split(h @ w_in, cuts, axis=-1)
    na_q = na_q.reshape(B, T, NA_HEADS, NA_HEAD_DIM)
    na_k = na_k.reshape(B, T, NA_HEADS, NA_HEAD_DIM)
    na_v = na_v.reshape(B, T, NA_HEADS, NA_HEAD_DIM)
    q = (rmsnorm(c_q, q_norm_g) @ w_uq).reshape(B, T, MLA_HEADS, MLA_NOPE_DIM + MLA_ROPE_DIM)
    q_nope, q_rope = q[..., :MLA_NOPE_DIM], q[..., MLA_NOPE_DIM:]
    kv = (rmsnorm(c_kv, kv_norm_g) @ w_ukv).reshape(B, T, MLA_HEADS, MLA_NOPE_DIM + MLA_V_DIM)
    k_nope, mla_v = kv[..., :MLA_NOPE_DIM], kv[..., MLA_NOPE_DIM:]
    return na_q, na_k, na_v, na_gate, q_nope, q_rope, k_nope, k_rope, mla_v, mla_gate


def dense_attention(q, k, v, scale):
    s = jnp.einsum('bqhd,bkhd->bhqk', q, k).astype(jnp.float32) * scale
    p = jax.nn.softmax(s, axis=-1)
    return jnp.einsum('bhqk,bkhd->bqhd', p.astype(v.dtype), v)


def neighbourhood_attention(q, k, v, k_ctx, v_ctx, rpb):
    B, S, H, D = q.shape
    rows = S // GRID_W
    wh = min(NA_WIN_H, rows)
    n_cb = GRID_W // NA_COL_BLOCK
    qg = q.reshape(B, rows, n_cb, NA_COL_BLOCK, H, D)
    kg = k.reshape(B, rows, GRID_W, H, D)
    vg = v.reshape(B, rows, GRID_W, H, D)
    q_cols = np.arange(GRID_W).reshape(n_cb, NA_COL_BLOCK)
    q_cs = np.clip(q_cols - NA_WIN_W // 2, 0, GRID_W - NA_WIN_W)
    cb_start = np.clip(np.arange(n_cb) * NA_COL_BLOCK - NA_WIN_W // 2, 0, GRID_W - NA_KEY_COLS)
    key_cols = cb_start[:, None] + np.arange(NA_KEY_COLS)[None]
    kc = key_cols[:, None, :]
    col_mask = (kc >= q_cs[:, :, None]) & (kc < q_cs[:, :, None] + NA_WIN_W)
    dcol = np.clip(kc - q_cols[:, :, None] + NA_WIN_W - 1, 0, 2 * NA_WIN_W - 2)
    bias_col = rpb[:, :, dcol]
    n_loc = wh * NA_KEY_COLS

    def row_block(r):
        rs = jnp.clip(r - wh // 2, 0, rows - wh)
        q_r = lax.dynamic_index_in_dim(qg, r, axis=1, keepdims=False)
        k_r = lax.dynamic_slice_in_dim(kg, rs, wh, axis=1)
        v_r = lax.dynamic_slice_in_dim(vg, rs, wh, axis=1)
        k_w = k_r[:, :, key_cols]
        v_w = v_r[:, :, key_cols]
        s_loc = jnp.einsum('bjqhd,bwjkhd->bhjqwk', q_r, k_w).astype(jnp.float32) * NA_SCALE
        drow = rs + jnp.arange(wh) - r + NA_WIN_H - 1
        bias = jnp.take(bias_col, drow, axis=1).transpose(0, 2, 3, 1, 4)
        s_loc = s_loc + bias[None].astype(jnp.float32)
        s_loc = jnp.where(col_mask[None, None, :, :, None, :], s_loc, -1e30)
        s_ctx = jnp.einsum('bjqhd,bchd->bhjqc', q_r, k_ctx).astype(jnp.float32) * NA_SCALE
        s = jnp.concatenate([s_loc.reshape(B, H, n_cb, NA_COL_BLOCK, n_loc), s_ctx], axis=-1)
        p = jax.nn.softmax(s, axis=-1).astype(v.dtype)
        p_loc = p[..., :n_loc].reshape(B, H, n_cb, NA_COL_BLOCK, wh, NA_KEY_COLS)
        p_ctx = p[..., n_loc:]
        return (jnp.einsum('bhjqwk,bwjkhd->bjqhd', p_loc, v_w)
                + jnp.einsum('bhjqc,bchd->bjqhd', p_ctx, v_ctx))

    out = lax.map(row_block, jnp.arange(rows))
    return jnp.moveaxis(out, 0, 1).reshape(B, S, H, D)


def mla_attend(q_nope, q_rope, k_nope, k_rope, v):
    s = (jnp.einsum('bqhd,bkhd->bhqk', q_nope, k_nope)
         + jnp.einsum('bqhr,bkr->bhqk', q_rope, k_rope)).astype(jnp.float32) * MLA_SCALE
    p = jax.nn.softmax(s, axis=-1)
    return jnp.einsum('bhqk,bkhd->bqhd', p.astype(v.dtype), v)


def mla_latent(q_nope, q_rope, k_nope, k_rope, v, kc_nope, kc_rope, vc):
    B, S, H, _ = q_nope.shape
    nb = S // MLA_Q_BLOCK
    kn = jnp.concatenate([k_nope, kc_nope], axis=1)
    kr = jnp.concatenate([k_rope, kc_rope], axis=1)
    vv = jnp.concatenate([v, vc], axis=1)
    qn = q_nope.reshape(B, nb, MLA_Q_BLOCK, H, MLA_NOPE_DIM).swapaxes(0, 1)
    qr = q_rope.reshape(B, nb, MLA_Q_BLOCK, H, MLA_ROPE_DIM).swapaxes(0, 1)
    o = lax.map(lambda qs: mla_attend(qs[0], qs[1], kn, kr, vv), (qn, qr))
    return o.swapaxes(0, 1).reshape(B, S, H, MLA_V_DIM)


def hybrid_layer(x, ctx, c, c_ctx, norm_g, w_ada, b_ada, w_in, rpb, q_norm_g, w_uq,
                 kv_norm_g, w_ukv, w_out, cos, sin, update_ctx):
    B, S, D = x.shape
    shift, scale, gate = jnp.split(jax.nn.silu(c) @ w_ada + b_ada, 3, axis=-1)
    shift_c, scale_c, gate_c = jnp.split(jax.nn.silu(c_ctx) @ w_ada + b_ada, 3, axis=-1)
    hx = rmsnorm(x, norm_g) * (1.0 + scale[:, None]) + shift[:, None]
    hc = rmsnorm(ctx, norm_g) * (1.0 + scale_c) + shift_c

    (na_q, na_k, na_v, na_gate, q_nope, q_rope, k_nope, k_rope, mla_v, mla_gate) = project(
        hx, w_in, q_norm_g, w_uq, kv_norm_g, w_ukv)
    (cna_q, cna_k, cna_v, cna_gate, cq_nope, cq_rope, ck_nope, ck_rope, cmla_v, cmla_gate) = project(
        hc, w_in, q_norm_g, w_uq, kv_norm_g, w_ukv)
    q_rope = apply_rope(q_rope, cos, sin)
    k_rope = apply_rope(k_rope, cos, sin)

    na_out = neighbourhood_attention(na_q, na_k, na_v, cna_k, cna_v, rpb)
    mla_out = mla_latent(q_nope, q_rope, k_nope, k_rope, mla_v, ck_nope, ck_rope, cmla_v)
    y = jnp.concatenate([na_out.reshape(B, S, NA_WIDTH) * jax.nn.silu(na_gate),
                         mla_out.reshape(B, S, MLA_WIDTH) * jax.nn.silu(mla_gate)], axis=-1) @ w_out
    x = x + gate[:, None] * y

    if update_ctx:
        C = ctx.shape[1]
        cna_out = dense_attention(cna_q, cna_k, cna_v, NA_SCALE)
        cmla_out = mla_attend(cq_nope, cq_rope, ck_nope, ck_rope, cmla_v)
        yc = jnp.concatenate([cna_out.reshape(B, C, NA_WIDTH) * jax.nn.silu(cna_gate),
                              cmla_out.reshape(B, C, MLA_WIDTH) * jax.nn.silu(cmla_gate)], axis=-1) @ w_out
        ctx = ctx + gate_c * yc
    return x, ctx


def setup_inputs(seed: int = 0) -> dict:
    key = jax.random.key(seed)
    ks = jax.random.split(key, 16)
    f32 = jnp.float32
    nrm = lambda k, shape, s: jax.random.normal(k, shape, f32) * s
    return {
        "x": nrm(ks[0], (BATCH, SEQ, D_MODEL), 1.0),
        "c": nrm(ks[1], (BATCH, D_MODEL), 1.0),
        "ctx": nrm(ks[2], (BATCH, CTX_LEN, D_MODEL), 1.0),
        "c_ctx": nrm(ks[3], (D_MODEL,), 1.0),
        "norm_g": 1.0 + nrm(ks[4], (DEPTH, D_MODEL), 0.01),
        "w_ada": nrm(ks[5], (DEPTH, D_MODEL, 3 * D_MODEL), D_MODEL ** -0.5),
        "b_ada": nrm(ks[6], (DEPTH, 3 * D_MODEL), 0.01),
        "w_in": nrm(ks[7], (DEPTH, D_MODEL, D_IN), D_MODEL ** -0.5),
        "na_rpb": nrm(ks[8], (DEPTH, NA_HEADS, 2 * NA_WIN_H - 1, 2 * NA_WIN_W - 1), 0.1),
        "q_norm_g": 1.0 + nrm(ks[9], (DEPTH, MLA_Q_RANK), 0.01),
        "w_uq": nrm(ks[10], (DEPTH, MLA_Q_RANK, MLA_HEADS * (MLA_NOPE_DIM + MLA_ROPE_DIM)), MLA_Q_RANK ** -0.5),
        "kv_norm_g": 1.0 + nrm(ks[11], (DEPTH, MLA_KV_RANK), 0.01),
        "w_ukv": nrm(ks[12], (DEPTH, MLA_KV_RANK, MLA_HEADS * (MLA_NOPE_DIM + MLA_V_DIM)), MLA_KV_RANK ** -0.5),
        "w_out": nrm(ks[13], (DEPTH, D_MIX, D_MODEL), D_MIX ** -0.5),
        "final_norm_g": 1.0 + nrm(ks[14], (D_MODEL,), 0.01),
    }


def reference(x, c, ctx, c_ctx, norm_g, w_ada, b_ada, w_in, na_rpb, q_norm_g, w_uq,
              kv_norm_g, w_ukv, w_out, final_norm_g):
    cos, sin = axial_rope_tables(x.shape[1])
    for l in range(DEPTH):
        x, ctx = hybrid_layer(x, ctx, c, c_ctx, norm_g[l], w_ada[l], b_ada[l], w_in[l], na_rpb[l],
                              q_norm_g[l], w_uq[l], kv_norm_g[l], w_ukv[l], w_out[l], cos, sin,
                              update_ctx=(l < DEPTH - 1))
    return rmsnorm(x, final_norm_g)
```

```python
import numpy as np
from contextlib import ExitStack
import concourse.bass as bass
import concourse.mybir as mybir
from concourse.bass_utils import run_bass_kernel_spmd

F32 = mybir.dt.float32
BF16 = mybir.dt.bfloat16
ALU = mybir.AluOpType
AF = mybir.ActivationFunctionType

N_CORES = 8
T = 2304
NCOL = 35328
WCH = 2208
NSMALL = 128
EPS = 1e-6
NA_SCALE = 64 ** -0.5
MLA_SCALE = 96 ** -0.5
N_DMA_SEM = 16


class Sched:
    def __init__(self, nc, es):
        self.nc = nc
        self.eng = {"pe": nc.tensor, "act": nc.scalar, "dve": nc.vector,
                    "pool": nc.gpsimd, "sp": nc.sync}
        self.sem = {}
        self.cnt = {}
        for e in ("pe", "act", "dve", "pool"):
            self.sem[e] = es.enter_context(nc.semaphore("s_" + e))
            self.cnt[e] = 0
        self.dsem = [es.enter_context(nc.semaphore("s_dma%d" % i)) for i in range(N_DMA_SEM)]
        self.dcnt = [0] * N_DMA_SEM
        self.dnext = 0
        self.waited = {e: {} for e in self.eng}
        self.lastw = {}
        self.reads = {}

    def _need(self, e, ev, lst):
        _, semobj, semid, val = ev
        w = self.waited[e]
        if w.get(semid, 0) >= val:
            return
        w[semid] = val
        for i, (so, sid, v) in enumerate(lst):
            if sid == semid:
                lst[i] = (so, sid, max(v, val))
                return
        lst.append((semobj, semid, val))

    def _wait(self, e, ev):
        lst = []
        self._need(e, ev, lst)
        for (so, sid, v) in lst:
            self.eng[e].wait_ge(so, v)

    def _deps(self, e, reads, writes, is_dma):
        lst = []
        for k in reads:
            ev = self.lastw.get(k)
            if ev is not None:
                self._need(e, ev, lst)
        strict = (e != "pe")
        for k in writes:
            ev = self.lastw.get(k)
            if ev is not None and (is_dma or strict or ev[0] != e):
                self._need(e, ev, lst)
            for rv in self.reads.get(k, ()):
                if is_dma or strict or rv[0] != e:
                    self._need(e, rv, lst)
        return lst

    def _record(self, ev, reads, writes):
        for k in writes:
            self.lastw[k] = ev
            self.reads[k] = []
        for k in reads:
            if k in writes:
                continue
            lst = self.reads.setdefault(k, [])
            lst[:] = [r for r in lst if r[2] != ev[2]]
            lst.append(ev)

    def op(self, e, fn, reads=(), writes=()):
        lst = self._deps(e, reads, writes, False)
        for (so, sid, v) in lst[:-1]:
            self.eng[e].wait_ge(so, v)
        ins = fn(self.eng[e])
        if lst:
            so, sid, v = lst[-1]
            ins._wait_ge(so, v)
        self.cnt[e] += 1
        ins.then_inc(self.sem[e], 1)
        self._record((e, self.sem[e], e, self.cnt[e]), reads, writes)
        return ins

    def dma(self, out, in_, reads=(), writes=(), q="sp"):
        s = self.dnext
        self.dnext = (self.dnext + 1) % N_DMA_SEM
        if self.dcnt[s] > 0:
            self._wait(q, ("dma", self.dsem[s], "d%d" % s, self.dcnt[s]))
        for (so, sid, v) in self._deps(q, reads, writes, True):
            self.eng[q].wait_ge(so, v)
        ins = self.eng[q].dma_start(out=out, in_=in_)
        self.dcnt[s] += 16
        ins.then_inc(self.dsem[s], 16)
        self._record(("dma", self.dsem[s], "d%d" % s, self.dcnt[s]), reads, writes)
        return ins

    def dma_barrier(self):
        for s in range(N_DMA_SEM):
            if self.dcnt[s] > 0:
                self._wait("sp", ("dma", self.dsem[s], "d%d" % s, self.dcnt[s]))

    def finish(self):
        self.dma_barrier()
        for e in ("pe", "act", "dve", "pool"):
            if self.cnt[e] > 0:
                self._wait("sp", (e, self.sem[e], e, self.cnt[e]))


def build(NB, NL=2):
    nc = bass.Bass("TRN2", target_bir_lowering=False)
    dram = nc.dram_tensor
    x_d = dram("x", [NB, 2048, 1024], F32, kind="ExternalInput").ap()
    ctx_d = dram("ctx", [NB, 256, 1024], F32, kind="ExternalInput").ap()
    wblob_d = dram("wblob", [2, 128, NCOL], F32, kind="ExternalInput").ap()
    wada_d = dram("wada", [48, 128, 1024], F32, kind="ExternalInput").ap()
    small_d = dram("small", [128, NSMALL], F32, kind="ExternalInput").ap()
    ident_d = dram("ident", [128, 128], F32, kind="ExternalInput").ap()
    cs_d = dram("cs", [2, 128, 2048], F32, kind="ExternalInput").ap()
    nab_d = dram("nab", [2, 8, 128, 2048], F32, kind="ExternalInput").ap()
    out_d = dram("out", [NB, 2048, 1024], F32, kind="ExternalOutput").ap()
    wbf_d = dram("wbf", [2, 128, NCOL], BF16, kind="Internal").ap()

    with ExitStack() as es:
        S = Sched(nc, es)
        sb = lambda n, sh, d: es.enter_context(nc.sbuf_tensor(n, sh, d))
        XT = sb("XT", [128, 8 * T], F32)
        HT = sb("HT", [128, 8 * T], BF16)
        AR = sb("AR", [128, 10 * T], BF16)
        VA = sb("VA", [128, 18 * 192], BF16)
        W = sb("W", [128, 4096], BF16)
        XS = [W[:, 0:2048].bitcast(F32), W[:, 2048:4096].bitcast(F32)]
        XSK = [["W0"], ["W1", "W2"]]
        WV = sb("WV", [128, 512], BF16)
        CS = sb("CS", [128, 4096], BF16)
        PT = [sb("PT%d" % i, [128, 512], BF16) for i in range(12)]
        TMP = [sb("TMP%d" % i, [128, 512], F32) for i in range(3)]
        RS = [sb("RS%d" % i, [128, 512], F32) for i in range(2)]
        SQ = [sb("SQ%d" % i, [128, 512], BF16) for i in range(4)]
        SM = sb("SM", [128, NSMALL], F32)
        SC = sb("SC", [128, 40], F32)
        MOD = sb("MOD", [128, 2 * 24 * 5], F32)
        GG = sb("GG", [128, 2 * 8 * 5], F32)
        IDT = sb("IDT", [128, 128], F32)
        ONES = sb("ONES", [128, 128], BF16)
        EPST = sb("EPST", [128, 1], F32)
        PS = [es.enter_context(nc.psum_tensor("ps%d" % i, [128, 512], F32)) for i in range(8)]

        XT3 = XT[:].rearrange("p (k t) -> p k t", t=T)
        HT3 = HT[:].rearrange("p (k t) -> p k t", t=T)
        VA3 = VA[:].rearrange("p (t c) -> p t c", c=192)

        def slot(i, n=1):
            return AR[:, i * T:(i + n) * T]

        def akeys(i, tbs=range(5)):
            return [("A", i, tb) for tb in tbs]

        roles = {"s": [0, 1, 2, 7], "a": [3, 4], "g": [5, 6, 7], "x": [5, 6, 7, 0, 1, 2], "a4": [3, 4, 5, 6], "s3": [0, 1, 2]}
        rptr = {"s": 0, "a": 0, "g": 0, "x": 0, "a4": 0, "s3": 0}

        def bank(role):
            lst = roles[role]
            i = lst[rptr[role] % len(lst)]
            rptr[role] += 1
            return i

        rot = {"PT": 0, "TMP": 0, "RS": 0, "SQ": 0, "XS": 0}

        def nxt(name, n):
            i = rot[name] % n
            rot[name] += 1
            return i

        blocks5 = [(tb, tb * 512, 512 if tb < 4 else 256) for tb in range(5)]

        def mod_ap(l, f, jj):
            i = (l * 24 + f) * 5 + jj
            return MOD[:, i:i + 1]

        def g_ap(l, kc, jj):
            i = (l * 8 + kc) * 5 + jj
            return GG[:, i:i + 1]

        S.dma(SM[:], small_d[:, :], writes=["SM"])
        S.dma(IDT[:], ident_d[:, :], writes=["IDT"])
        S.op("dve", lambda e: e.memset(ONES[:], 1.0), writes=["ONES"])
        S.op("dve", lambda e: e.memset(EPST[:], EPS), writes=["EPST"])
        S.op("pool", lambda e: e.memset(VA3[:, :, 64:128], 1.0), writes=[("VA", t) for t in range(18)])
        for t in range(2):
            S.dma(XT[:, t * 2048:(t + 1) * 2048], cs_d[t, :, :], writes=[("stg", t)])
            S.op("dve", lambda e: e.tensor_copy(out=CS[:, t * 2048:(t + 1) * 2048], in_=XT[:, t * 2048:(t + 1) * 2048]),
                 reads=[("stg", t)], writes=["CS"])
        S.op("act", lambda e: e.activation(out=SC[:], in_=SM[:, 0:40], func=AF.Silu), reads=["SM"], writes=["SC"])
        SC3 = SC[:].rearrange("p (k j) -> p k j", j=5)
        MOD4 = MOD[:].rearrange("p (l f j) -> p l f j", l=2, f=24)
        GG4 = GG[:].rearrange("p (l k j) -> p l k j", l=2, k=8)
        def cast_chunk(ci):
            l, ch = ci // 16, ci % 16
            bi = ci % 4
            si = 4 + bi
            stg = XT[:, 8192 + bi * WCH: 8192 + (bi + 1) * WCH]
            stb = HT[:, bi * WCH:(bi + 1) * WCH]
            S.dma(stg, wblob_d[l, :, ch * WCH:(ch + 1) * WCH], writes=[("stg", si)])
            if ci % 2 == 1:
                S.op("act", lambda e: e.activation(out=stb, in_=stg, func=AF.Identity), reads=[("stg", si)], writes=[("stb", bi)])
            else:
                S.op("dve", lambda e: e.tensor_copy(out=stb, in_=stg), reads=[("stg", si)], writes=[("stb", bi)])
            S.dma(wbf_d[l, :, ch * WCH:(ch + 1) * WCH], stb, reads=[("stb", bi)])

        ncast = 16 * NL
        cdone = 0
        for l in range(NL):
            pa = bank("x")
            for fc in range(24):
                si = 2 + (fc % 2)
                stg = XT[:, 4096 + (fc % 2) * 1024: 4096 + (fc % 2 + 1) * 1024]
                S.dma(stg, wada_d[l * 24 + fc, :, :], writes=[("stg", si)])
                for kc in range(8):
                    S.op("pe", lambda e: e.matmul(PS[pa][:, fc * 5:(fc + 1) * 5], lhsT=stg[:, kc * 128:(kc + 1) * 128],
                                                  rhs=SC3[:, kc, :], start=(kc == 0), stop=(kc == 7)),
                         reads=[("stg", si), "SC"], writes=[("ps", pa)])
                want = ((l * 24 + fc + 1) * ncast) // (24 * NL)
                while cdone < want:
                    cast_chunk(cdone)
                    cdone += 1
            pa3 = PS[pa][:, 0:120].rearrange("p (f j) -> p f j", j=5)
            for jj in range(5):
                S.op("dve", lambda e: e.tensor_tensor(out=MOD4[:, l, :, jj], in0=pa3[:, :, jj],
                                                      in1=SM[:, 40 + l * 24: 40 + (l + 1) * 24], op=ALU.add),
                     reads=[("ps", pa), "SM"], writes=["MOD"])
            for jj in range(5):
                S.op("dve", lambda e: e.scalar_tensor_tensor(out=GG4[:, l, :, jj], in0=MOD4[:, l, 8:16, jj], scalar=1.0,
                                                             in1=SM[:, 88 + l * 8: 88 + (l + 1) * 8],
                                                             op0=ALU.add, op1=ALU.mult),
                     reads=["MOD", "SM"], writes=["GG"])
        while cdone < ncast:
            cast_chunk(cdone)
            cdone += 1
        S.dma_barrier()

        def rstd_from(ssb, n, scale):
            ri = nxt("RS", 2)
            S.op("act", lambda e: e.activation(out=RS[ri][:, :n], in_=PS[ssb][:, :n], func=AF.Ln, bias=EPST[:, 0:1], scale=scale),
                 reads=[("ps", ssb), "EPST"], writes=[("RS", ri)])
            S.op("act", lambda e: e.activation(out=RS[ri][:, :n], in_=RS[ri][:, :n], func=AF.Exp, scale=-0.5), reads=[("RS", ri)], writes=[("RS", ri)])
            return ri

        def load_x(b):
            for tt in range(18):
                xi = nxt("XS", 2)
                src = x_d[b, tt * 128:(tt + 1) * 128, :] if tt < 16 else ctx_d[b, (tt - 16) * 128:(tt - 15) * 128, :]
                S.dma(XS[xi], src, writes=XSK[xi])
                for half in range(2):
                    pb_ = bank("x")
                    for k4 in range(4):
                        kc = half * 4 + k4
                        S.op("pe", lambda e: e.transpose(out=PS[pb_][:, k4 * 128:(k4 + 1) * 128], in_=XS[xi][:, kc * 128:(kc + 1) * 128], identity=IDT[:]),
                             reads=XSK[xi] + ["IDT"], writes=[("ps", pb_)])
                    eng = "dve" if half == 0 else "act"
                    src3 = PS[pb_][:, :].rearrange("p (k t) -> p k t", t=128)
                    dst3 = XT3[:, half * 4:(half + 1) * 4, tt * 128:(tt + 1) * 128]
                    if eng == "dve":
                        S.op("dve", lambda e: e.tensor_copy(out=dst3, in_=src3), reads=[("ps", pb_)], writes=[("XT", tt // 4)])
                    else:
                        S.op("act", lambda e: e.activation(out=dst3, in_=src3, func=AF.Identity), reads=[("ps", pb_)], writes=[("XT", tt // 4)])

        def sumsq_blocks(tb, c0, n):
            ssb = bank("s3")
            for kc in range(8):
                qi = nxt("SQ", 4)
                S.op("pool" if kc % 2 == 0 else "dve",
                     lambda e: e.tensor_tensor(out=SQ[qi][:, :n], in0=XT3[:, kc, c0:c0 + n], in1=XT3[:, kc, c0:c0 + n], op=ALU.mult),
                     reads=[("XT", tb)], writes=[("SQ", qi)])
                S.op("pe", lambda e: e.matmul(PS[ssb][:, :n], lhsT=ONES[:], rhs=SQ[qi][:, :n], start=(kc == 0), stop=(kc == 7)),
                     reads=[("SQ", qi), "ONES"], writes=[("ps", ssb)])
            return ssb

        def norm_mod(l, b):
            for tb, c0, n in blocks5:
                jj = b if tb < 4 else 4
                ssb = sumsq_blocks(tb, c0, n)
                ri = rstd_from(ssb, n, 1.0 / 1024)
                for kc in range(8):
                    ti = nxt("TMP", 3)
                    S.op("dve", lambda e: e.scalar_tensor_tensor(out=TMP[ti][:, :n], in0=XT3[:, kc, c0:c0 + n], scalar=g_ap(l, kc, jj),
                                                                 in1=RS[ri][:, :n], op0=ALU.mult, op1=ALU.mult),
                         reads=[("XT", tb), ("RS", ri), "GG"], writes=[("TMP", ti)])
                    S.op("act", lambda e: e.activation(out=HT3[:, kc, c0:c0 + n], in_=TMP[ti][:, :n], func=AF.Identity,
                                                       bias=mod_ap(l, kc, jj), scale=1.0),
                         reads=[("TMP", ti), "MOD"], writes=[("HT", tb)])

        def proj_fm(wcols, dst_slot, blks, evac, M=128, wkey=("W0",), w3=None):
            for tb, c0, n in blks:
                pb_ = bank("x")
                for kc in range(8):
                    S.op("pe", lambda e: e.matmul(PS[pb_][0:M, :n], lhsT=w3[:, kc, wcols[0]:wcols[1]], rhs=HT3[:, kc, c0:c0 + n],
                                                  start=(kc == 0), stop=(kc == 7)),
                         reads=list(wkey) + [("HT", tb)], writes=[("ps", pb_)])
                evac(pb_, tb, c0, n)

        def copy_evac(dst_slot_i, eng, M=128):
            def f(pb_, tb, c0, n):
                dst = slot(dst_slot_i)[0:M, c0:c0 + n]
                if eng == "act":
                    S.op("act", lambda e: e.activation(out=dst, in_=PS[pb_][0:M, :n], func=AF.Identity),
                         reads=[("ps", pb_)], writes=[("A", dst_slot_i, tb)])
                else:
                    S.op(eng, lambda e: e.tensor_copy(out=dst, in_=PS[pb_][0:M, :n]),
                         reads=[("ps", pb_)], writes=[("A", dst_slot_i, tb)])
            return f

        def silu_evac(dst_slot_i):
            def f(pb_, tb, c0, n):
                S.op("act", lambda e: e.activation(out=slot(dst_slot_i)[:, c0:c0 + n], in_=PS[pb_][:, :n], func=AF.Silu),
                     reads=[("ps", pb_)], writes=[("A", dst_slot_i, tb)])
            return f

        def va_evac(pb_, tt):
            S.op("dve", lambda e: e.tensor_copy(out=VA3[:, tt, 0:64], in_=PS[pb_][:, 0:64]), reads=[("ps", pb_)], writes=[("VA", tt)])
            S.op("act", lambda e: e.activation(out=VA3[:, tt, 128:192], in_=PS[pb_][:, 64:128], func=AF.Identity), reads=[("ps", pb_)], writes=[("VA", tt)])

        def normalize(ab, e_, n, at_dst, g_src, at_keys, g_keys, shape3=None, mode="dve"):
            nm = slice(64 * e_, 64 * e_ + 64)
            dn = slice(64 * (1 - e_), 64 * (1 - e_) + 64)
            t1 = nxt("TMP", 3)
            if mode == "dve":
                S.op("dve", lambda e: e.reciprocal(out=TMP[t1][dn, :n], in_=PS[ab][dn, :n]), reads=[("ps", ab)], writes=[("TMP", t1)])
            else:
                S.op("act", lambda e: e.activation(out=TMP[t1][dn, :n], in_=PS[ab][dn, :n], func=AF.Ln), reads=[("ps", ab)], writes=[("TMP", t1)])
                S.op("act", lambda e: e.activation(out=TMP[t1][dn, :n], in_=TMP[t1][dn, :n], func=AF.Exp, scale=-1.0), reads=[("TMP", t1)], writes=[("TMP", t1)])
            t2 = nxt("TMP", 3)
            S.op("dve", lambda e: e.tensor_tensor(out=TMP[t2][nm, :n], in0=PS[ab][nm, :n], in1=TMP[t1][dn, :n], op=ALU.mult),
                 reads=[("ps", ab), ("TMP", t1)], writes=[("TMP", t2)])
            src = TMP[t2][nm, :n]
            if shape3 is not None:
                src = src.rearrange("p (r c) -> p r c", c=shape3)
            S.op("pool", lambda e: e.tensor_tensor(out=at_dst, in0=src, in1=g_src, op=ALU.mult),
                 reads=[("TMP", t2)] + g_keys, writes=at_keys)

        WK_ALL = ["W0", "W1", "W2"]

        def out_proj(l, b, blks, chunks):
            nch = len(chunks)
            for tb, c0, n in blks:
                jj = b if tb < 4 else 4
                for dm in range(8):
                    pb_ = bank("x")
                    for ci_, (w_o, wkeys, asl) in enumerate(chunks):
                        S.op("pe", lambda e: e.matmul(PS[pb_][:, :n], lhsT=w_o[:, dm * 128:(dm + 1) * 128], rhs=slot(asl)[:, c0:c0 + n],
                                                      start=(ci_ == 0), stop=(ci_ == nch - 1)),
                             reads=wkeys + [("A", asl, tb)], writes=[("ps", pb_)])
                    S.op("dve", lambda e: e.scalar_tensor_tensor(out=XT3[:, dm, c0:c0 + n], in0=PS[pb_][:, :n], scalar=mod_ap(l, 16 + dm, jj),
                                                                 in1=XT3[:, dm, c0:c0 + n], op0=ALU.mult, op1=ALU.add),
                         reads=[("ps", pb_), ("XT", tb), "MOD"], writes=[("XT", tb)])

        pend = []
        DEPTH = 10

        deferred = []
        DEFER = 4

        def _pv_one():
            tl, pi = pend.pop(0)
            tl["pv"](pi)
            if tl.get("done") is not None:
                deferred.append([DEFER, tl["done"]])

        def _tick():
            for d in deferred:
                d[0] -= 1
            while deferred and deferred[0][0] <= 0:
                deferred.pop(0)[1]()

        def attn_add(tl):
            sbk = bank("s")
            tl["qk"](sbk)
            pi = nxt("PT", 12)
            n = tl["n"]
            sc = tl["scale"]
            S.op("act", lambda e: e.activation(out=PT[pi][:, :n], in_=PS[sbk][:, :n], func=AF.Exp, scale=sc),
                 reads=[("ps", sbk)], writes=[("PT", pi)])
            if tl.get("post") is not None:
                tl["post"](pi)
            pend.append((tl, pi))
            if len(pend) > DEPTH:
                _pv_one()
            _tick()

        def attn_flush():
            while pend:
                _pv_one()
            while deferred:
                deferred.pop(0)[1]()

        def mk_qk(lhsT, rhs, n, rkeys):
            def qk(sbk):
                S.op("pe", lambda e: e.matmul(PS[sbk][:, :n], lhsT=lhsT, rhs=rhs, start=True, stop=True),
                     reads=rkeys, writes=[("ps", sbk)])
            return qk

        def mk_pv(ab, o0, n, lhsT, vkey, first, lastt):
            def pv(pi):
                S.op("pe", lambda e: e.matmul(PS[ab][:, o0:o0 + n], lhsT=lhsT, rhs=PT[pi][:, :n], start=first, stop=lastt),
                     reads=[vkey, ("PT", pi)], writes=[("ps", ab)])
            return pv

        NA_AT = [6, 7, 8, 9]

        def na_pair(l, b, c):
            last = (l == NL - 1)
            qblks = blocks5[:4] if last else blocks5
            base = c * 5120
            S.dma(W[:, 0:4096], wbf_d[l, :, base:base + 4096], writes=WK_ALL)
            W3 = W[:, 0:4096].rearrange("p (k f) -> p k f", f=512)
            QBD = slot(0, 2)
            QBD3 = QBD.rearrange("p (e t) -> p e t", t=T)
            QBD4 = QBD3[:, :, 0:2048].rearrange("p e (r c) -> p e r c", c=64)
            KT, GT = slot(2), slot(3)
            if c == 0:
                S.op("pool", lambda e: e.memset(QBD3[64:128, 0, :], 0.0), writes=akeys(0))
                S.op("pool", lambda e: e.memset(QBD3[0:64, 1, :], 0.0), writes=akeys(1))
            for e_ in range(2):
                for piece in range(4):
                    ti = nxt("TMP", 3)
                    S.dma(TMP[ti][:], nab_d[l, 2 * c + e_, :, piece * 512:(piece + 1) * 512], writes=[("TMP", ti)])
                    S.op("act", lambda e: e.activation(out=slot(4 + e_)[:, piece * 512:(piece + 1) * 512], in_=TMP[ti][:], func=AF.Exp),
                         reads=[("TMP", ti)], writes=akeys(4 + e_))
            EB2 = slot(4, 2).rearrange("p (e t) -> p e t", t=T)

            def q_evac(pb_, tb, c0, n):
                S.op("dve", lambda e: e.tensor_copy(out=QBD3[0:64, 0, c0:c0 + n], in_=PS[pb_][0:64, :n]), reads=[("ps", pb_)], writes=[("A", 0, tb)])
                S.op("act", lambda e: e.activation(out=QBD3[64:128, 1, c0:c0 + n], in_=PS[pb_][64:128, :n], func=AF.Identity), reads=[("ps", pb_)], writes=[("A", 1, tb)])
            proj_fm((0, 128), 0, qblks, q_evac, w3=W3, wkey=WK_ALL)
            proj_fm((128, 256), 2, blocks5, copy_evac(2, "act"), w3=W3, wkey=WK_ALL)
            proj_fm((384, 512), 3, qblks, silu_evac(3), w3=W3, wkey=WK_ALL)
            for tt in range(18):
                pb_ = bank("x")
                for kc in range(8):
                    S.op("pe", lambda e: e.matmul(PS[pb_][:, 0:128], lhsT=HT3[:, kc, tt * 128:(tt + 1) * 128], rhs=W3[:, kc, 256:384],
                                                  start=(kc == 0), stop=(kc == 7)),
                         reads=WK_ALL + [("HT", tt // 4)], writes=[("ps", pb_)])
                va_evac(pb_, tt)
            ats = NA_AT[c]
            AT = slot(ats)
            G3 = GT[:, 0:2048].rearrange("p (r c) -> p r c", c=64)
            A3 = AT[:, 0:2048].rearrange("p (r c) -> p r c", c=64)
            lat = range(4)
            qkeys_lat = akeys(0, lat) + akeys(1, lat)
            for j in range(4):
                abs_ = [bank("a4"), bank("a4")]
                tiles = []
                for ck in range(2):
                    for e_ in range(2):
                        first = (ck == 0)
                        tiles.append(dict(
                            qk=mk_qk(KT[:, 2048 + ck * 128: 2048 + (ck + 1) * 128], QBD4[:, e_, :, 16 * j:16 * j + 16], 512,
                                     akeys(2, [4]) + qkeys_lat),
                            n=512, scale=NA_SCALE, post=None,
                            pv=mk_pv(abs_[e_], 0, 512, VA3[:, 16 + ck, 64 * e_:64 * e_ + 128], ("VA", 16 + ck), first, False)))
                for kt in range(16):
                    r0 = 0 if kt <= 3 else 2 * kt - 3
                    r1 = 31 if kt >= 12 else 2 * kt + 5
                    nq = (r1 - r0 + 1) * 16
                    if kt <= 3:
                        segs = [(1, 0, 3), (0, 4, r1)]
                    elif kt >= 12:
                        segs = [(0, r0, 28), (1, 29, 31)]
                    else:
                        segs = [(0, r0, r1)]
                    lastt = (kt == 15)

                    def post(pi, kt=kt, r0=r0, nq=nq, segs=segs, j=j):
                        P3 = PT[pi][:, 0:2 * nq].rearrange("p (e q) -> p e q", q=nq)
                        for (v, ra, rb) in segs:
                            d_a = 7 - 2 * kt + ra
                            nn = (rb - ra + 1) * 16
                            pc = (ra - r0) * 16
                            ec = v * 1024 + j * 256 + d_a * 16
                            S.op("dve", lambda e: e.tensor_tensor(out=P3[:, :, pc:pc + nn], in0=P3[:, :, pc:pc + nn],
                                                                  in1=EB2[:, :, ec:ec + nn], op=ALU.mult),
                                 reads=[("PT", pi)] + akeys(4) + akeys(5), writes=[("PT", pi)])

                    def pv(pi, kt=kt, r0=r0, nq=nq, lastt=lastt, abs_=abs_):
                        for e_ in range(2):
                            S.op("pe", lambda e: e.matmul(PS[abs_[e_]][:, r0 * 16:r0 * 16 + nq], lhsT=VA3[:, kt, 64 * e_:64 * e_ + 128],
                                                          rhs=PT[pi][:, e_ * nq:(e_ + 1) * nq], start=False, stop=lastt),
                                 reads=[("VA", kt), ("PT", pi)], writes=[("ps", abs_[e_])])
                    tl = dict(qk=mk_qk(KT[:, kt * 128:(kt + 1) * 128], QBD4[:, :, r0:r1 + 1, 16 * j:16 * j + 16], 2 * nq,
                                       akeys(2, [kt // 4]) + qkeys_lat),
                              n=2 * nq, scale=NA_SCALE, post=post, pv=pv)
                    if lastt:
                        def done(abs_=abs_, j=j):
                            for e_ in range(2):
                                pr = slice(64 * e_, 64 * e_ + 64)
                                normalize(abs_[e_], e_, 512, A3[pr, :, 16 * j:16 * j + 16], G3[pr, :, 16 * j:16 * j + 16],
                                          akeys(ats, lat), akeys(3, lat), shape3=16, mode="act")
                        tl["done"] = done
                    tiles.append(tl)
                for tl in tiles:
                    attn_add(tl)
            if not last:
                for e_ in range(2):
                    pr = slice(64 * e_, 64 * e_ + 64)
                    ab = bank("a4")
                    for ck in range(2):
                        tl = dict(qk=mk_qk(KT[:, 2048 + ck * 128: 2048 + (ck + 1) * 128], QBD3[:, e_, 2048:2304], 256,
                                           akeys(2, [4]) + akeys(0, [4]) + akeys(1, [4])),
                                  n=256, scale=NA_SCALE, post=None,
                                  pv=mk_pv(ab, 0, 256, VA3[:, 16 + ck, 64 * e_:64 * e_ + 128], ("VA", 16 + ck), ck == 0, ck == 1))
                        if ck == 1:
                            def done(ab=ab, e_=e_, pr=pr):
                                normalize(ab, e_, 256, AT[pr, 2048:2304], GT[pr, 2048:2304], akeys(ats, [4]), akeys(3, [4]), mode="act")
                            tl["done"] = done
                        attn_add(tl)
            attn_flush()
            if c == 3:
                chunks = []
                for cc in range(4):
                    S.dma(W[:, cc * 1024:(cc + 1) * 1024], wbf_d[l, :, cc * 5120 + 4096:cc * 5120 + 5120], writes=WK_ALL)
                    chunks.append((W[:, cc * 1024:(cc + 1) * 1024], WK_ALL, NA_AT[cc]))
                out_proj(l, b, qblks, chunks)

        def rope_evac(pa, dst, c0, n, dkeys, latent):
            rr = slice(64, 96)
            r2 = slice(96, 128)
            if latent:
                t1 = nxt("TMP", 3)
                S.op("dve", lambda e: e.tensor_tensor(out=TMP[t1][rr, :n], in0=PS[pa][rr, :n], in1=CS[rr, c0:c0 + n], op=ALU.mult),
                     reads=[("ps", pa), "CS"], writes=[("TMP", t1)])
                t2 = nxt("TMP", 3)
                S.op("dve", lambda e: e.tensor_tensor(out=TMP[t2][rr, :n], in0=PS[pa][r2, :n], in1=CS[r2, 2048 + c0:2048 + c0 + n], op=ALU.mult),
                     reads=[("ps", pa), "CS"], writes=[("TMP", t2)])
                S.op("pool", lambda e: e.tensor_tensor(out=dst[rr, c0:c0 + n], in0=TMP[t1][rr, :n], in1=TMP[t2][rr, :n], op=ALU.add),
                     reads=[("TMP", t1), ("TMP", t2)], writes=dkeys)
            else:
                S.op("dve", lambda e: e.tensor_copy(out=dst[rr, c0:c0 + n], in_=PS[pa][rr, :n]), reads=[("ps", pa)], writes=dkeys)

        def mla_shared(l, b):
            base = 20480
            S.dma(W[:, 0:3584], wbf_d[l, :, base:base + 3584], writes=WK_ALL)
            S.dma(WV[:], wbf_d[l, :, base + 3584:base + 4096], writes=["Wv"])
            W3 = W[:, 0:3584].rearrange("p (k f) -> p k f", f=448)
            CQN3 = slot(4, 2).rearrange("p (m t) -> p m t", t=T)
            CKVN = slot(6)
            for tb, c0, n in blocks5:
                pcq = [bank("g"), bank("g")]
                pkv = bank("g")
                pka = bank("a")
                for kc in range(8):
                    st, sp_ = (kc == 0), (kc == 7)
                    rd = WK_ALL + [("HT", tb)]
                    for m in range(2):
                        S.op("pe", lambda e: e.matmul(PS[pcq[m]][:, :n], lhsT=W3[:, kc, m * 128:(m + 1) * 128], rhs=HT3[:, kc, c0:c0 + n], start=st, stop=sp_),
                             reads=rd, writes=[("ps", pcq[m])])
                    S.op("pe", lambda e: e.matmul(PS[pkv][:, :n], lhsT=W3[:, kc, 256:384], rhs=HT3[:, kc, c0:c0 + n], start=st, stop=sp_),
                         reads=rd, writes=[("ps", pkv)])
                    S.op("pe", lambda e: e.matmul(PS[pka][64:128, :n], lhsT=W3[:, kc, 384:448], rhs=HT3[:, kc, c0:c0 + n], start=st, stop=sp_, tile_position=(0, 64)),
                         reads=rd, writes=[("ps", pka)])
                ssq = bank("s3")
                for m in range(2):
                    qi = nxt("SQ", 4)
                    S.op("act", lambda e: e.activation(out=SQ[qi][:, :n], in_=PS[pcq[m]][:, :n], func=AF.Square), reads=[("ps", pcq[m])], writes=[("SQ", qi)])
                    S.op("pe", lambda e: e.matmul(PS[ssq][:, :n], lhsT=ONES[:], rhs=SQ[qi][:, :n], start=(m == 0), stop=(m == 1)),
                         reads=[("SQ", qi), "ONES"], writes=[("ps", ssq)])
                ri = rstd_from(ssq, n, 1.0 / 256)
                for m in range(2):
                    S.op("dve", lambda e: e.scalar_tensor_tensor(out=CQN3[:, m, c0:c0 + n], in0=PS[pcq[m]][:, :n], scalar=SM[:, 112 + l * 2 + m:113 + l * 2 + m],
                                                                 in1=RS[ri][:, :n], op0=ALU.mult, op1=ALU.mult),
                         reads=[("ps", pcq[m]), ("RS", ri), "SM"], writes=[("A", 4 + m, tb)])
                ssk = bank("s3")
                qi = nxt("SQ", 4)
                S.op("act", lambda e: e.activation(out=SQ[qi][:, :n], in_=PS[pkv][:, :n], func=AF.Square), reads=[("ps", pkv)], writes=[("SQ", qi)])
                S.op("pe", lambda e: e.matmul(PS[ssk][:, :n], lhsT=ONES[:], rhs=SQ[qi][:, :n], start=True, stop=True),
                     reads=[("SQ", qi), "ONES"], writes=[("ps", ssk)])
                ri = rstd_from(ssk, n, 1.0 / 128)
                S.op("dve", lambda e: e.scalar_tensor_tensor(out=CKVN[:, c0:c0 + n], in0=PS[pkv][:, :n], scalar=SM[:, 116 + l:117 + l],
                                                             in1=RS[ri][:, :n], op0=ALU.mult, op1=ALU.mult),
                     reads=[("ps", pkv), ("RS", ri), "SM"], writes=[("A", 6, tb)])
                rope_evac(pka, slot(2), c0, n, [("A", 2, tb)], tb < 4)

        MLA_AT = [7, 8, 1, 9]

        def mla_pair(l, b, c):
            last = (l == NL - 1)
            qblks = blocks5[:4] if last else blocks5
            base = 24576 + c * 2688
            S.dma(W[:, 0:1664], wbf_d[l, :, base:base + 1664], writes=["W0"])
            WG3 = W[:, 0:1024].rearrange("p (k f) -> p k f", f=128)
            WQ3 = W[:, 1024:1536].rearrange("p (m f) -> p m f", f=256)
            WK = W[:, 1536:1664]
            CQN3 = slot(4, 2).rearrange("p (m t) -> p m t", t=T)
            CKVN = slot(6)
            ats = MLA_AT[c]
            QM, KM, GT, AT = slot(0), slot(2), slot(3), slot(ats)
            proj_fm((0, 128), 3, qblks, silu_evac(3), w3=WG3, wkey=["W0"])
            for tt in range(18):
                pb_ = bank("x")
                S.op("pe", lambda e: e.matmul(PS[pb_][:, 0:128], lhsT=CKVN[:, tt * 128:(tt + 1) * 128], rhs=WV[:, c * 128:(c + 1) * 128], start=True, stop=True),
                     reads=["Wv", ("A", 6, tt // 4)], writes=[("ps", pb_)])
                va_evac(pb_, tt)
            for e_ in range(2):
                nm = slice(64 * e_, 64 * e_ + 64)
                for tb, c0, n in qblks:
                    p1 = bank("x")
                    for m in range(2):
                        S.op("pe", lambda e: e.matmul(PS[p1][:, :n], lhsT=WQ3[:, m, e_ * 128:(e_ + 1) * 128], rhs=CQN3[:, m, c0:c0 + n], start=(m == 0), stop=(m == 1)),
                             reads=["W0", ("A", 4 + m, tb)], writes=[("ps", p1)])
                    if tb % 2 == 0:
                        S.op("act", lambda e: e.activation(out=QM[0:64, c0:c0 + n], in_=PS[p1][0:64, :n], func=AF.Identity), reads=[("ps", p1)], writes=[("A", 0, tb)])
                    else:
                        S.op("dve", lambda e: e.tensor_copy(out=QM[0:64, c0:c0 + n], in_=PS[p1][0:64, :n]), reads=[("ps", p1)], writes=[("A", 0, tb)])
                    rope_evac(p1, QM, c0, n, [("A", 0, tb)], tb < 4)
                for tb, c0, n in blocks5:
                    pk = bank("x")
                    S.op("pe", lambda e: e.matmul(PS[pk][0:64, :n], lhsT=WK[:, e_ * 64:(e_ + 1) * 64], rhs=CKVN[:, c0:c0 + n], start=True, stop=True),
                         reads=["W0", ("A", 6, tb)], writes=[("ps", pk)])
                    if tb % 2 == 1:
                        S.op("act", lambda e: e.activation(out=KM[0:64, c0:c0 + n], in_=PS[pk][0:64, :n], func=AF.Identity), reads=[("ps", pk)], writes=[("A", 2, tb)])
                    else:
                        S.op("dve", lambda e: e.tensor_copy(out=KM[0:64, c0:c0 + n], in_=PS[pk][0:64, :n]), reads=[("ps", pk)], writes=[("A", 2, tb)])
                if c == 3 and e_ == 1:
                    for cc in range(4):
                        bs = 24576 + cc * 2688 + 1664
                        S.dma(W[:, cc * 1024:(cc + 1) * 1024], wbf_d[l, :, bs:bs + 1024], writes=WK_ALL)
                for tb, c0, n in qblks:
                    ab = bank("a")
                    kcs = list(range(18)) if tb < 4 else [16, 17]
                    for i_, kc in enumerate(kcs):
                        first, lastt = (i_ == 0), (i_ == len(kcs) - 1)
                        tl = dict(qk=mk_qk(KM[0:96, kc * 128:(kc + 1) * 128], QM[0:96, c0:c0 + n], n, [("A", 2, kc // 4), ("A", 0, tb)]),
                                  n=n, scale=MLA_SCALE, post=None,
                                  pv=mk_pv(ab, 0, n, VA3[:, kc, 64 * e_:64 * e_ + 128], ("VA", kc), first, lastt))
                        if lastt:
                            def done(ab=ab, e_=e_, n=n, c0=c0, tb=tb, nm=nm):
                                normalize(ab, e_, n, AT[nm, c0:c0 + n], GT[nm, c0:c0 + n], [("A", ats, tb)], [("A", 3, tb)], mode="dve")
                            tl["done"] = done
                        attn_add(tl)
                attn_flush()
            if c == 3:
                chunks = [(W[:, cc * 1024:(cc + 1) * 1024], WK_ALL, MLA_AT[cc]) for cc in range(4)]
                out_proj(l, b, qblks, chunks)

        def final_out(b):
            for tb, c0, n in blocks5[:4]:
                ssb = sumsq_blocks(tb, c0, n)
                ri = rstd_from(ssb, n, 1.0 / 1024)
                for kc in range(8):
                    S.op("dve", lambda e: e.scalar_tensor_tensor(out=XT3[:, kc, c0:c0 + n], in0=XT3[:, kc, c0:c0 + n], scalar=SM[:, 104 + kc:105 + kc],
                                                                 in1=RS[ri][:, :n], op0=ALU.mult, op1=ALU.mult),
                         reads=[("XT", tb), ("RS", ri), "SM"], writes=[("XT", tb)])
                for t4 in range(4):
                    tt = tb * 4 + t4
                    xi = nxt("XS", 2)
                    for half in range(2):
                        pb_ = bank("x")
                        for k4 in range(4):
                            kc = half * 4 + k4
                            S.op("pe", lambda e: e.transpose(out=PS[pb_][:, k4 * 128:(k4 + 1) * 128], in_=XT3[:, kc, tt * 128:(tt + 1) * 128], identity=IDT[:]),
                                 reads=[("XT", tb), "IDT"], writes=[("ps", pb_)])
                        if half == 0:
                            S.op("dve", lambda e: e.tensor_copy(out=XS[xi][:, 0:512], in_=PS[pb_][:, :]), reads=[("ps", pb_)], writes=XSK[xi])
                        else:
                            S.op("act", lambda e: e.activation(out=XS[xi][:, 512:1024], in_=PS[pb_][:, :], func=AF.Identity), reads=[("ps", pb_)], writes=XSK[xi])
                    S.dma(out_d[b, tt * 128:(tt + 1) * 128, :], XS[xi], reads=XSK[xi])

        for b in range(NB):
            load_x(b)
            for l in range(NL):
                norm_mod(l, b)
                for c in range(4):
                    na_pair(l, b, c)
                mla_shared(l, b)
                for c in range(4):
                    mla_pair(l, b, c)
            final_out(b)
        S.finish()
    return nc


def _prep_shared(inp):
    w_in, w_out, w_uq, w_ukv = (np.asarray(inp[k], np.float32) for k in ("w_in", "w_out", "w_uq", "w_ukv"))
    blob = np.zeros((2, 128, NCOL), np.float32)
    sw = np.arange(32) ^ 1
    for l in range(2):
        wi = w_in[l].reshape(8, 128, 2976)
        wo = w_out[l].reshape(8, 128, 1024)
        for c in range(4):
            base = c * 5120
            sel = np.stack([wi[:, :, s * 512 + c * 128: s * 512 + (c + 1) * 128] for s in range(4)], axis=2)
            blob[l, :, base:base + 4096] = sel.transpose(1, 0, 2, 3).reshape(128, 4096)
            blob[l, :, base + 4096:base + 5120] = wo[c]
        base = 20480
        cols = np.concatenate([np.arange(2048, 2432), 2432 + np.arange(32), 2432 + sw])
        blob[l, :, base:base + 3584] = wi[:, :, cols].transpose(1, 0, 2).reshape(128, 3584)
        vcols = (np.arange(8)[:, None] * 128 + 64 + np.arange(64)[None]).reshape(-1)
        blob[l, :, base + 3584:base + 4096] = w_ukv[l][:, vcols]
        wq = w_uq[l].reshape(2, 128, 768)
        for c in range(4):
            base = 24576 + c * 2688
            blob[l, :, base:base + 1024] = wi[:, :, 2464 + c * 128:2464 + (c + 1) * 128].transpose(1, 0, 2).reshape(128, 1024)
            qc = []
            for e in range(2):
                h = 2 * c + e
                qc += [h * 96 + np.arange(64), h * 96 + 64 + np.arange(32), h * 96 + 64 + sw]
            qc = np.concatenate(qc)
            blob[l, :, base + 1024:base + 1536] = wq[:, :, qc].transpose(1, 0, 2).reshape(128, 512)
            kc_ = np.concatenate([(2 * c + e) * 128 + np.arange(64) for e in range(2)])
            blob[l, :, base + 1536:base + 1664] = w_ukv[l][:, kc_]
            blob[l, :, base + 1664:base + 2688] = wo[4 + c]
    w_ada = np.asarray(inp["w_ada"], np.float32)
    wada = np.stack([w_ada[l].reshape(8, 128, 24, 128).transpose(2, 1, 0, 3).reshape(24, 128, 1024) for l in range(2)]).reshape(48, 128, 1024)
    t = np.arange(2048)
    row = (t // 64).astype(np.float32)
    col = (t % 64).astype(np.float32)
    inv = (1.0 / (np.float32(10000.0) ** (np.arange(0, 16, 2, dtype=np.float32) / np.float32(16)))).astype(np.float32)
    ang = np.concatenate([row[:, None] * inv[None], col[:, None] * inv[None]], axis=-1).astype(np.float32)
    f = np.arange(32)
    cosT = np.cos(ang.astype(np.float64))[:, f // 2].T
    sinT = np.sin(ang.astype(np.float64))[:, f // 2].T * np.where(f % 2 == 0, -1.0, 1.0)[:, None]
    cs = np.zeros((2, 128, 2048), np.float32)
    cs[0, 64:96] = cosT
    cs[1, 64:96] = sinT
    cs[1, 96:128] = sinT
    rpb = np.asarray(inp["na_rpb"], np.float32)
    p = np.arange(128)
    kr = (p // 64)[:, None, None, None, None]
    kcc = (p % 64)[:, None, None, None, None]
    v = np.arange(2)[None, :, None, None, None]
    j = np.arange(4)[None, None, :, None, None]
    d = np.arange(16)[None, None, None, :, None]
    qc_ = np.arange(16)[None, None, None, None, :]
    dr = 7 + kr - d
    q = 16 * j + qc_
    q_cs = np.clip(q - 8, 0, 48)
    valid = (np.abs(dr) <= 7) & ((v == 1) | ((dr >= -4) & (dr <= 3))) & (kcc >= q_cs) & (kcc < q_cs + 16)
    valid = np.broadcast_to(valid, (128, 2, 4, 16, 16))
    ir = np.broadcast_to(np.clip(dr + 7, 0, 14), (128, 2, 4, 16, 16))
    ic = np.broadcast_to(np.clip(kcc - q + 15, 0, 30), (128, 2, 4, 16, 16))
    g = rpb[:, :, ir, ic]
    nab = np.where(valid[None, None], g, np.float32(-30000.0)).astype(np.float32).reshape(2, 8, 128, 2048)
    return dict(wblob=blob, wada=np.ascontiguousarray(wada), ident=np.eye(128, dtype=np.float32), cs=cs, nab=np.ascontiguousarray(nab))


def _prep_small(inp, bidx):
    c = np.asarray(inp["c"], np.float32)
    cc = np.concatenate([c[bidx], np.asarray(inp["c_ctx"], np.float32)[None]], axis=0)
    if cc.shape[0] < 5:
        cc = np.concatenate([cc[:-1], np.zeros((5 - cc.shape[0], 1024), np.float32), cc[-1:]], axis=0)
    sm = np.zeros((128, NSMALL), np.float32)
    sm[:, 0:40] = cc.reshape(5, 8, 128).transpose(2, 1, 0).reshape(128, 40)
    sm[:, 40:88] = np.asarray(inp["b_ada"], np.float32).reshape(2, 24, 128).transpose(2, 0, 1).reshape(128, 48)
    sm[:, 88:104] = np.asarray(inp["norm_g"], np.float32).reshape(2, 8, 128).transpose(2, 0, 1).reshape(128, 16)
    sm[:, 104:112] = np.asarray(inp["final_norm_g"], np.float32).reshape(8, 128).T
    sm[:, 112:116] = np.asarray(inp["q_norm_g"], np.float32).reshape(2, 2, 128).transpose(2, 0, 1).reshape(128, 4)
    sm[:, 116:118] = np.asarray(inp["kv_norm_g"], np.float32).T
    return sm


_NC_CACHE = {}


def _get_nc(nb):
    if nb not in _NC_CACHE:
        _NC_CACHE[nb] = build(nb)
    return _NC_CACHE[nb]


def run_cores(inp, batch_lists):
    nb = len(batch_lists[0])
    shared = _prep_shared(inp)
    x = np.asarray(inp["x"], np.float32)
    ctx = np.asarray(inp["ctx"], np.float32)
    in_maps = []
    for bl in batch_lists:
        m = dict(shared)
        m["x"] = np.ascontiguousarray(x[bl])
        m["ctx"] = np.ascontiguousarray(ctx[bl])
        m["small"] = _prep_small(inp, bl)
        in_maps.append(m)
    nc = _get_nc(nb)
    res = run_bass_kernel_spmd(nc, in_maps, core_ids=list(range(len(batch_lists))))
    return [r["out"] for r in res.results]


def kernel(**inputs):
    B = inputs["x"].shape[0]
    per = B // N_CORES
    bls = [list(range(i * per, (i + 1) * per)) for i in range(N_CORES)]
    outs = run_cores(inputs, bls)
    return np.concatenate(outs, axis=0).astype(np.float32)
```

```python
import numpy as np
from contextlib import ExitStack
import concourse.bass as bass
import concourse.mybir as mybir
from concourse.bass_utils import run_bass_kernel_spmd

F32 = mybir.dt.float32
BF16 = mybir.dt.bfloat16
ALU = mybir.AluOpType
AF = mybir.ActivationFunctionType

N_CORES = 8
T = 2304
NCOL = 35328
WCH = 2208
NSMALL = 128
EPS = 1e-6
NA_SCALE = 64 ** -0.5
MLA_SCALE = 96 ** -0.5
N_DMA_SEM = 16


class Sched:
    def __init__(self, nc, es):
        self.nc = nc
        self.eng = {"pe": nc.tensor, "act": nc.scalar, "dve": nc.vector,
                    "pool": nc.gpsimd, "sp": nc.sync}
        self.sem = {}
        self.cnt = {}
        for e in ("pe", "act", "dve", "pool"):
            self.sem[e] = es.enter_context(nc.semaphore("s_" + e))
            self.cnt[e] = 0
        self.dsem = [es.enter_context(nc.semaphore("s_dma%d" % i)) for i in range(N_DMA_SEM)]
        self.dcnt = [0] * N_DMA_SEM
        self.dnext = 0
        self.waited = {e: {} for e in self.eng}
        self.lastw = {}
        self.reads = {}

    def _need(self, e, ev, lst):
        _, semobj, semid, val = ev
        w = self.waited[e]
        if w.get(semid, 0) >= val:
            return
        w[semid] = val
        for i, (so, sid, v) in enumerate(lst):
            if sid == semid:
                lst[i] = (so, sid, max(v, val))
                return
        lst.append((semobj, semid, val))

    def _wait(self, e, ev):
        lst = []
        self._need(e, ev, lst)
        for (so, sid, v) in lst:
            self.eng[e].wait_ge(so, v)

    def _deps(self, e, reads, writes, is_dma):
        lst = []
        for k in reads:
            ev = self.lastw.get(k)
            if ev is not None:
                self._need(e, ev, lst)
        strict = (e != "pe")
        for k in writes:
            ev = self.lastw.get(k)
            if ev is not None and (is_dma or strict or ev[0] != e):
                self._need(e, ev, lst)
            for rv in self.reads.get(k, ()):
                if is_dma or strict or rv[0] != e:
                    self._need(e, rv, lst)
        return lst

    def _record(self, ev, reads, writes):
        for k in writes:
            self.lastw[k] = ev
            self.reads[k] = []
        for k in reads:
            if k in writes:
                continue
            lst = self.reads.setdefault(k, [])
            lst[:] = [r for r in lst if r[2] != ev[2]]
            lst.append(ev)

    def op(self, e, fn, reads=(), writes=()):
        lst = self._deps(e, reads, writes, False)
        for (so, sid, v) in lst[:-1]:
            self.eng[e].wait_ge(so, v)
        ins = fn(self.eng[e])
        if lst:
            so, sid, v = lst[-1]
            ins._wait_ge(so, v)
        self.cnt[e] += 1
        ins.then_inc(self.sem[e], 1)
        self._record((e, self.sem[e], e, self.cnt[e]), reads, writes)
        return ins

    def dma(self, out, in_, reads=(), writes=(), q="sp"):
        s = self.dnext
        self.dnext = (self.dnext + 1) % N_DMA_SEM
        if self.dcnt[s] > 0:
            self._wait(q, ("dma", self.dsem[s], "d%d" % s, self.dcnt[s]))
        for (so, sid, v) in self._deps(q, reads, writes, True):
            self.eng[q].wait_ge(so, v)
        ins = self.eng[q].dma_start(out=out, in_=in_)
        self.dcnt[s] += 16
        ins.then_inc(self.dsem[s], 16)
        self._record(("dma", self.dsem[s], "d%d" % s, self.dcnt[s]), reads, writes)
        return ins

    def dma_barrier(self):
        for s in range(N_DMA_SEM):
            if self.dcnt[s] > 0:
                self._wait("sp", ("dma", self.dsem[s], "d%d" % s, self.dcnt[s]))

    def finish(self):
        self.dma_barrier()
        for e in ("pe", "act", "dve", "pool"):
            if self.cnt[e] > 0:
                self._wait("sp", (e, self.sem[e], e, self.cnt[e]))


def build(NB, NL=2):
    nc = bass.Bass("TRN2", target_bir_lowering=False)
    dram = nc.dram_tensor
    x_d = dram("x", [NB, 2048, 1024], F32, kind="ExternalInput").ap()
    ctx_d = dram("ctx", [NB, 256, 1024], F32, kind="ExternalInput").ap()
    wblob_d = dram("wblob", [2, 128, NCOL], F32, kind="ExternalInput").ap()
    wada_d = dram("wada", [48, 128, 1024], F32, kind="ExternalInput").ap()
    small_d = dram("small", [128, NSMALL], F32, kind="ExternalInput").ap()
    ident_d = dram("ident", [128, 128], F32, kind="ExternalInput").ap()
    cs_d = dram("cs", [2, 128, 2048], F32, kind="ExternalInput").ap()
    nab_d = dram("nab", [2, 8, 128, 2048], F32, kind="ExternalInput").ap()
    out_d = dram("out", [NB, 2048, 1024], F32, kind="ExternalOutput").ap()
    wbf_d = dram("wbf", [2, 128, NCOL], BF16, kind="Internal").ap()

    with ExitStack() as es:
        S = Sched(nc, es)
        sb = lambda n, sh, d: es.enter_context(nc.sbuf_tensor(n, sh, d))
        XT = sb("XT", [128, 8 * T], F32)
        HT = sb("HT", [128, 8 * T], BF16)
        AR = sb("AR", [128, 10 * T], BF16)
        VA = sb("VA", [128, 18 * 192], BF16)
        W = sb("W", [128, 4096], BF16)
        XS = [W[:, 0:2048].bitcast(F32), W[:, 2048:4096].bitcast(F32)]
        XSK = [["W0"], ["W1", "W2"]]
        WV = sb("WV", [128, 512], BF16)
        CS = sb("CS", [128, 4096], BF16)
        PT = [sb("PT%d" % i, [128, 512], BF16) for i in range(10)]
        TMP = [sb("TMP%d" % i, [128, 512], F32) for i in range(3)]
        RS = [sb("RS%d" % i, [128, 512], F32) for i in range(2)]
        SQ = [sb("SQ%d" % i, [128, 512], BF16) for i in range(4)]
        SM = sb("SM", [128, NSMALL], F32)
        SC = sb("SC", [128, 40], F32)
        MOD = sb("MOD", [128, 2 * 24 * 5], F32)
        GG = sb("GG", [128, 2 * 8 * 5], F32)
        IDT = sb("IDT", [128, 128], F32)
        ONES = sb("ONES", [128, 128], BF16)
        EPST = sb("EPST", [128, 1], F32)
        PS = [es.enter_context(nc.psum_tensor("ps%d" % i, [128, 512], F32)) for i in range(8)]

        XT3 = XT[:].rearrange("p (k t) -> p k t", t=T)
        HT3 = HT[:].rearrange("p (k t) -> p k t", t=T)
        VA3 = VA[:].rearrange("p (t c) -> p t c", c=192)

        def slot(i, n=1):
            return AR[:, i * T:(i + n) * T]

        def akeys(i, tbs=range(5)):
            return [("A", i, tb) for tb in tbs]

        roles = {"s": [0, 1, 2, 7], "a": [3, 4], "g": [5, 6, 7], "x": [5, 6, 7, 0, 1, 2], "a4": [3, 4, 5, 6], "s3": [0, 1, 2]}
        rptr = {"s": 0, "a": 0, "g": 0, "x": 0, "a4": 0, "s3": 0}

        def bank(role):
            lst = roles[role]
            i = lst[rptr[role] % len(lst)]
            rptr[role] += 1
            return i

        rot = {"PT": 0, "TMP": 0, "RS": 0, "SQ": 0, "XS": 0}

        def nxt(name, n):
            i = rot[name] % n
            rot[name] += 1
            return i

        blocks5 = [(tb, tb * 512, 512 if tb < 4 else 256) for tb in range(5)]

        def mod_ap(l, f, jj):
            i = (l * 24 + f) * 5 + jj
            return MOD[:, i:i + 1]

        def g_ap(l, kc, jj):
            i = (l * 8 + kc) * 5 + jj
            return GG[:, i:i + 1]

        S.dma(SM[:], small_d[:, :], writes=["SM"])
        S.dma(IDT[:], ident_d[:, :], writes=["IDT"])
        S.op("dve", lambda e: e.memset(ONES[:], 1.0), writes=["ONES"])
        S.op("dve", lambda e: e.memset(EPST[:], EPS), writes=["EPST"])
        S.op("pool", lambda e: e.memset(VA3[:, :, 64:128], 1.0), writes=[("VA", t) for t in range(18)])
        for t in range(2):
            S.dma(XT[:, t * 2048:(t + 1) * 2048], cs_d[t, :, :], writes=[("stg", t)])
            S.op("dve", lambda e: e.tensor_copy(out=CS[:, t * 2048:(t + 1) * 2048], in_=XT[:, t * 2048:(t + 1) * 2048]),
                 reads=[("stg", t)], writes=["CS"])
        S.op("act", lambda e: e.activation(out=SC[:], in_=SM[:, 0:40], func=AF.Silu), reads=["SM"], writes=["SC"])
        SC3 = SC[:].rearrange("p (k j) -> p k j", j=5)
        MOD4 = MOD[:].rearrange("p (l f j) -> p l f j", l=2, f=24)
        GG4 = GG[:].rearrange("p (l k j) -> p l k j", l=2, k=8)
        def cast_chunk(ci):
            l, ch = ci // 16, ci % 16
            bi = ci % 4
            si = 4 + bi
            stg = XT[:, 8192 + bi * WCH: 8192 + (bi + 1) * WCH]
            stb = HT[:, bi * WCH:(bi + 1) * WCH]
            S.dma(stg, wblob_d[l, :, ch * WCH:(ch + 1) * WCH], writes=[("stg", si)])
            if ci % 2 == 1:
                S.op("act", lambda e: e.activation(out=stb, in_=stg, func=AF.Identity), reads=[("stg", si)], writes=[("stb", bi)])
            else:
                S.op("dve", lambda e: e.tensor_copy(out=stb, in_=stg), reads=[("stg", si)], writes=[("stb", bi)])
            S.dma(wbf_d[l, :, ch * WCH:(ch + 1) * WCH], stb, reads=[("stb", bi)])

        ncast = 16 * NL
        cdone = 0
        for l in range(NL):
            pa = bank("x")
            for fc in range(24):
                si = 2 + (fc % 2)
                stg = XT[:, 4096 + (fc % 2) * 1024: 4096 + (fc % 2 + 1) * 1024]
                S.dma(stg, wada_d[l * 24 + fc, :, :], writes=[("stg", si)], q="act")
                for kc in range(8):
                    S.op("pe", lambda e: e.matmul(PS[pa][:, fc * 5:(fc + 1) * 5], lhsT=stg[:, kc * 128:(kc + 1) * 128],
                                                  rhs=SC3[:, kc, :], start=(kc == 0), stop=(kc == 7)),
                         reads=[("stg", si), "SC"], writes=[("ps", pa)])
                want = ((l * 24 + fc + 1) * ncast) // (24 * NL)
                while cdone < want:
                    cast_chunk(cdone)
                    cdone += 1
            pa3 = PS[pa][:, 0:120].rearrange("p (f j) -> p f j", j=5)
            for jj in range(5):
                S.op("dve", lambda e: e.tensor_tensor(out=MOD4[:, l, :, jj], in0=pa3[:, :, jj],
                                                      in1=SM[:, 40 + l * 24: 40 + (l + 1) * 24], op=ALU.add),
                     reads=[("ps", pa), "SM"], writes=["MOD"])
            for jj in range(5):
                S.op("dve", lambda e: e.scalar_tensor_tensor(out=GG4[:, l, :, jj], in0=MOD4[:, l, 8:16, jj], scalar=1.0,
                                                             in1=SM[:, 88 + l * 8: 88 + (l + 1) * 8],
                                                             op0=ALU.add, op1=ALU.mult),
                     reads=["MOD", "SM"], writes=["GG"])
        while cdone < ncast:
            cast_chunk(cdone)
            cdone += 1
        S.dma_barrier()

        def rstd_from(ssb, n, scale):
            ri = nxt("RS", 2)
            S.op("act", lambda e: e.activation(out=RS[ri][:, :n], in_=PS[ssb][:, :n], func=AF.Ln, bias=EPST[:, 0:1], scale=scale),
                 reads=[("ps", ssb), "EPST"], writes=[("RS", ri)])
            S.op("act", lambda e: e.activation(out=RS[ri][:, :n], in_=RS[ri][:, :n], func=AF.Exp, scale=-0.5), reads=[("RS", ri)], writes=[("RS", ri)])
            return ri

        def load_x(b):
            for tt in range(18):
                xi = nxt("XS", 2)
                src = x_d[b, tt * 128:(tt + 1) * 128, :] if tt < 16 else ctx_d[b, (tt - 16) * 128:(tt - 15) * 128, :]
                S.dma(XS[xi], src, writes=XSK[xi])
                for half in range(2):
                    pb_ = bank("x")
                    for k4 in range(4):
                        kc = half * 4 + k4
                        S.op("pe", lambda e: e.transpose(out=PS[pb_][:, k4 * 128:(k4 + 1) * 128], in_=XS[xi][:, kc * 128:(kc + 1) * 128], identity=IDT[:]),
                             reads=XSK[xi] + ["IDT"], writes=[("ps", pb_)])
                    eng = "dve" if half == 0 else "act"
                    src3 = PS[pb_][:, :].rearrange("p (k t) -> p k t", t=128)
                    dst3 = XT3[:, half * 4:(half + 1) * 4, tt * 128:(tt + 1) * 128]
                    if eng == "dve":
                        S.op("dve", lambda e: e.tensor_copy(out=dst3, in_=src3), reads=[("ps", pb_)], writes=[("XT", tt // 4)])
                    else:
                        S.op("act", lambda e: e.activation(out=dst3, in_=src3, func=AF.Identity), reads=[("ps", pb_)], writes=[("XT", tt // 4)])

        def sumsq_blocks(tb, c0, n):
            ssb = bank("s3")
            for kc in range(8):
                qi = nxt("SQ", 4)
                S.op("pool" if kc % 2 == 0 else "dve",
                     lambda e: e.tensor_tensor(out=SQ[qi][:, :n], in0=XT3[:, kc, c0:c0 + n], in1=XT3[:, kc, c0:c0 + n], op=ALU.mult),
                     reads=[("XT", tb)], writes=[("SQ", qi)])
                S.op("pe", lambda e: e.matmul(PS[ssb][:, :n], lhsT=ONES[:], rhs=SQ[qi][:, :n], start=(kc == 0), stop=(kc == 7)),
                     reads=[("SQ", qi), "ONES"], writes=[("ps", ssb)])
            return ssb

        def norm_mod(l, b):
            for tb, c0, n in blocks5:
                jj = b if tb < 4 else 4
                ssb = sumsq_blocks(tb, c0, n)
                ri = rstd_from(ssb, n, 1.0 / 1024)
                for kc in range(8):
                    ti = nxt("TMP", 3)
                    S.op("dve", lambda e: e.scalar_tensor_tensor(out=TMP[ti][:, :n], in0=XT3[:, kc, c0:c0 + n], scalar=g_ap(l, kc, jj),
                                                                 in1=RS[ri][:, :n], op0=ALU.mult, op1=ALU.mult),
                         reads=[("XT", tb), ("RS", ri), "GG"], writes=[("TMP", ti)])
                    S.op("act", lambda e: e.activation(out=HT3[:, kc, c0:c0 + n], in_=TMP[ti][:, :n], func=AF.Identity,
                                                       bias=mod_ap(l, kc, jj), scale=1.0),
                         reads=[("TMP", ti), "MOD"], writes=[("HT", tb)])

        def proj_fm(wcols, dst_slot, blks, evac, M=128, wkey=("W0",), w3=None):
            for tb, c0, n in blks:
                pb_ = bank("x")
                for kc in range(8):
                    S.op("pe", lambda e: e.matmul(PS[pb_][0:M, :n], lhsT=w3[:, kc, wcols[0]:wcols[1]], rhs=HT3[:, kc, c0:c0 + n],
                                                  start=(kc == 0), stop=(kc == 7)),
                         reads=list(wkey) + [("HT", tb)], writes=[("ps", pb_)])
                evac(pb_, tb, c0, n)

        def copy_evac(dst_slot_i, eng, M=128):
            def f(pb_, tb, c0, n):
                dst = slot(dst_slot_i)[0:M, c0:c0 + n]
                if eng == "act":
                    S.op("act", lambda e: e.activation(out=dst, in_=PS[pb_][0:M, :n], func=AF.Identity),
                         reads=[("ps", pb_)], writes=[("A", dst_slot_i, tb)])
                else:
                    S.op(eng, lambda e: e.tensor_copy(out=dst, in_=PS[pb_][0:M, :n]),
                         reads=[("ps", pb_)], writes=[("A", dst_slot_i, tb)])
            return f

        def silu_evac(dst_slot_i):
            def f(pb_, tb, c0, n):
                S.op("act", lambda e: e.activation(out=slot(dst_slot_i)[:, c0:c0 + n], in_=PS[pb_][:, :n], func=AF.Silu),
                     reads=[("ps", pb_)], writes=[("A", dst_slot_i, tb)])
            return f

        def va_evac(pb_, tt):
            S.op("dve", lambda e: e.tensor_copy(out=VA3[:, tt, 0:64], in_=PS[pb_][:, 0:64]), reads=[("ps", pb_)], writes=[("VA", tt)])
            S.op("act", lambda e: e.activation(out=VA3[:, tt, 128:192], in_=PS[pb_][:, 64:128], func=AF.Identity), reads=[("ps", pb_)], writes=[("VA", tt)])

        def normalize(ab, e_, n, at_dst, g_src, at_keys, g_keys, shape3=None, mode="dve"):
            nm = slice(64 * e_, 64 * e_ + 64)
            dn = slice(64 * (1 - e_), 64 * (1 - e_) + 64)
            t1 = nxt("TMP", 3)
            if mode == "dve":
                S.op("dve", lambda e: e.reciprocal(out=TMP[t1][dn, :n], in_=PS[ab][dn, :n]), reads=[("ps", ab)], writes=[("TMP", t1)])
            else:
                S.op("act", lambda e: e.activation(out=TMP[t1][dn, :n], in_=PS[ab][dn, :n], func=AF.Ln), reads=[("ps", ab)], writes=[("TMP", t1)])
                S.op("act", lambda e: e.activation(out=TMP[t1][dn, :n], in_=TMP[t1][dn, :n], func=AF.Exp, scale=-1.0), reads=[("TMP", t1)], writes=[("TMP", t1)])
            t2 = nxt("TMP", 3)
            S.op("dve", lambda e: e.tensor_tensor(out=TMP[t2][nm, :n], in0=PS[ab][nm, :n], in1=TMP[t1][dn, :n], op=ALU.mult),
                 reads=[("ps", ab), ("TMP", t1)], writes=[("TMP", t2)])
            src = TMP[t2][nm, :n]
            if shape3 is not None:
                src = src.rearrange("p (r c) -> p r c", c=shape3)
            S.op("pool", lambda e: e.tensor_tensor(out=at_dst, in0=src, in1=g_src, op=ALU.mult),
                 reads=[("TMP", t2)] + g_keys, writes=at_keys)

        WK_ALL = ["W0", "W1", "W2"]

        def out_proj(l, b, blks, chunks):
            nch = len(chunks)
            for tb, c0, n in blks:
                jj = b if tb < 4 else 4
                for dm in range(8):
                    pb_ = bank("x")
                    for ci_, (w_o, wkeys, asl) in enumerate(chunks):
                        S.op("pe", lambda e: e.matmul(PS[pb_][:, :n], lhsT=w_o[:, dm * 128:(dm + 1) * 128], rhs=slot(asl)[:, c0:c0 + n],
                                                      start=(ci_ == 0), stop=(ci_ == nch - 1)),
                             reads=wkeys + [("A", asl, tb)], writes=[("ps", pb_)])
                    S.op("dve", lambda e: e.scalar_tensor_tensor(out=XT3[:, dm, c0:c0 + n], in0=PS[pb_][:, :n], scalar=mod_ap(l, 16 + dm, jj),
                                                                 in1=XT3[:, dm, c0:c0 + n], op0=ALU.mult, op1=ALU.add),
                         reads=[("ps", pb_), ("XT", tb), "MOD"], writes=[("XT", tb)])

        pend = []
        DEPTH = 8

        deferred = []
        DEFER = 4

        def _pv_one():
            tl, pi = pend.pop(0)
            tl["pv"](pi)
            if tl.get("done") is not None:
                deferred.append([DEFER, tl["done"]])

        def _tick():
            for d in deferred:
                d[0] -= 1
            while deferred and deferred[0][0] <= 0:
                deferred.pop(0)[1]()

        def attn_add(tl):
            sbk = bank("s")
            tl["qk"](sbk)
            pi = nxt("PT", 10)
            n = tl["n"]
            sc = tl["scale"]
            S.op("act", lambda e: e.activation(out=PT[pi][:, :n], in_=PS[sbk][:, :n], func=AF.Exp, scale=sc),
                 reads=[("ps", sbk)], writes=[("PT", pi)])
            if tl.get("post") is not None:
                tl["post"](pi)
            pend.append((tl, pi))
            if len(pend) > DEPTH:
                _pv_one()
            _tick()

        def attn_flush():
            while pend:
                _pv_one()
            while deferred:
                deferred.pop(0)[1]()

        def mk_qk(lhsT, rhs, n, rkeys):
            def qk(sbk):
                S.op("pe", lambda e: e.matmul(PS[sbk][:, :n], lhsT=lhsT, rhs=rhs, start=True, stop=True),
                     reads=rkeys, writes=[("ps", sbk)])
            return qk

        def mk_pv(ab, o0, n, lhsT, vkey, first, lastt):
            def pv(pi):
                S.op("pe", lambda e: e.matmul(PS[ab][:, o0:o0 + n], lhsT=lhsT, rhs=PT[pi][:, :n], start=first, stop=lastt),
                     reads=[vkey, ("PT", pi)], writes=[("ps", ab)])
            return pv

        NA_AT = [6, 7, 8, 9]

        def na_pair(l, b, c):
            last = (l == NL - 1)
            qblks = blocks5[:4] if last else blocks5
            base = c * 5120
            S.dma(W[:, 0:4096], wbf_d[l, :, base:base + 4096], writes=WK_ALL)
            W3 = W[:, 0:4096].rearrange("p (k f) -> p k f", f=512)
            QBD = slot(0, 2)
            QBD3 = QBD.rearrange("p (e t) -> p e t", t=T)
            QBD4 = QBD3[:, :, 0:2048].rearrange("p e (r c) -> p e r c", c=64)
            KT, GT = slot(2), slot(3)
            if c == 0:
                S.op("pool", lambda e: e.memset(QBD3[64:128, 0, :], 0.0), writes=akeys(0))
                S.op("pool", lambda e: e.memset(QBD3[0:64, 1, :], 0.0), writes=akeys(1))
            for e_ in range(2):
                for piece in range(4):
                    ti = nxt("TMP", 3)
                    S.dma(TMP[ti][:], nab_d[l, 2 * c + e_, :, piece * 512:(piece + 1) * 512], writes=[("TMP", ti)])
                    S.op("act", lambda e: e.activation(out=slot(4 + e_)[:, piece * 512:(piece + 1) * 512], in_=TMP[ti][:], func=AF.Exp),
                         reads=[("TMP", ti)], writes=akeys(4 + e_))
            EB2 = slot(4, 2).rearrange("p (e t) -> p e t", t=T)

            def q_evac(pb_, tb, c0, n):
                S.op("dve", lambda e: e.tensor_copy(out=QBD3[0:64, 0, c0:c0 + n], in_=PS[pb_][0:64, :n]), reads=[("ps", pb_)], writes=[("A", 0, tb)])
                S.op("act", lambda e: e.activation(out=QBD3[64:128, 1, c0:c0 + n], in_=PS[pb_][64:128, :n], func=AF.Identity), reads=[("ps", pb_)], writes=[("A", 1, tb)])
            proj_fm((0, 128), 0, qblks, q_evac, w3=W3, wkey=WK_ALL)
            proj_fm((128, 256), 2, blocks5, copy_evac(2, "act"), w3=W3, wkey=WK_ALL)
            proj_fm((384, 512), 3, qblks, silu_evac(3), w3=W3, wkey=WK_ALL)
            for tt in range(18):
                pb_ = bank("x")
                for kc in range(8):
                    S.op("pe", lambda e: e.matmul(PS[pb_][:, 0:128], lhsT=HT3[:, kc, tt * 128:(tt + 1) * 128], rhs=W3[:, kc, 256:384],
                                                  start=(kc == 0), stop=(kc == 7)),
                         reads=WK_ALL + [("HT", tt // 4)], writes=[("ps", pb_)])
                va_evac(pb_, tt)
            ats = NA_AT[c]
            AT = slot(ats)
            G3 = GT[:, 0:2048].rearrange("p (r c) -> p r c", c=64)
            A3 = AT[:, 0:2048].rearrange("p (r c) -> p r c", c=64)
            lat = range(4)
            qkeys_lat = akeys(0, lat) + akeys(1, lat)
            for j in range(4):
                abs_ = [bank("a4"), bank("a4")]
                tiles = []
                for ck in range(2):
                    for e_ in range(2):
                        first = (ck == 0)
                        tiles.append(dict(
                            qk=mk_qk(KT[:, 2048 + ck * 128: 2048 + (ck + 1) * 128], QBD4[:, e_, :, 16 * j:16 * j + 16], 512,
                                     akeys(2, [4]) + qkeys_lat),
                            n=512, scale=NA_SCALE, post=None,
                            pv=mk_pv(abs_[e_], 0, 512, VA3[:, 16 + ck, 64 * e_:64 * e_ + 128], ("VA", 16 + ck), first, False)))
                for kt in range(16):
                    r0 = 0 if kt <= 3 else 2 * kt - 3
                    r1 = 31 if kt >= 12 else 2 * kt + 5
                    nq = (r1 - r0 + 1) * 16
                    if kt <= 3:
                        segs = [(1, 0, 3), (0, 4, r1)]
                    elif kt >= 12:
                        segs = [(0, r0, 28), (1, 29, 31)]
                    else:
                        segs = [(0, r0, r1)]
                    lastt = (kt == 15)

                    def post(pi, kt=kt, r0=r0, nq=nq, segs=segs, j=j):
                        P3 = PT[pi][:, 0:2 * nq].rearrange("p (e q) -> p e q", q=nq)
                        for (v, ra, rb) in segs:
                            d_a = 7 - 2 * kt + ra
                            nn = (rb - ra + 1) * 16
                            pc = (ra - r0) * 16
                            ec = v * 1024 + j * 256 + d_a * 16
                            S.op("dve", lambda e: e.tensor_tensor(out=P3[:, :, pc:pc + nn], in0=P3[:, :, pc:pc + nn],
                                                                  in1=EB2[:, :, ec:ec + nn], op=ALU.mult),
                                 reads=[("PT", pi)] + akeys(4) + akeys(5), writes=[("PT", pi)])

                    def pv(pi, kt=kt, r0=r0, nq=nq, lastt=lastt, abs_=abs_):
                        for e_ in range(2):
                            S.op("pe", lambda e: e.matmul(PS[abs_[e_]][:, r0 * 16:r0 * 16 + nq], lhsT=VA3[:, kt, 64 * e_:64 * e_ + 128],
                                                          rhs=PT[pi][:, e_ * nq:(e_ + 1) * nq], start=False, stop=lastt),
                                 reads=[("VA", kt), ("PT", pi)], writes=[("ps", abs_[e_])])
                    tl = dict(qk=mk_qk(KT[:, kt * 128:(kt + 1) * 128], QBD4[:, :, r0:r1 + 1, 16 * j:16 * j + 16], 2 * nq,
                                       akeys(2, [kt // 4]) + qkeys_lat),
                              n=2 * nq, scale=NA_SCALE, post=post, pv=pv)
                    if lastt:
                        def done(abs_=abs_, j=j):
                            for e_ in range(2):
                                pr = slice(64 * e_, 64 * e_ + 64)
                                normalize(abs_[e_], e_, 512, A3[pr, :, 16 * j:16 * j + 16], G3[pr, :, 16 * j:16 * j + 16],
                                          akeys(ats, lat), akeys(3, lat), shape3=16, mode="act")
                        tl["done"] = done
                    tiles.append(tl)
                for tl in tiles:
                    attn_add(tl)
            if not last:
                for e_ in range(2):
                    pr = slice(64 * e_, 64 * e_ + 64)
                    ab = bank("a4")
                    for ck in range(2):
                        tl = dict(qk=mk_qk(KT[:, 2048 + ck * 128: 2048 + (ck + 1) * 128], QBD3[:, e_, 2048:2304], 256,
                                           akeys(2, [4]) + akeys(0, [4]) + akeys(1, [4])),
                                  n=256, scale=NA_SCALE, post=None,
                                  pv=mk_pv(ab, 0, 256, VA3[:, 16 + ck, 64 * e_:64 * e_ + 128], ("VA", 16 + ck), ck == 0, ck == 1))
                        if ck == 1:
                            def done(ab=ab, e_=e_, pr=pr):
                                normalize(ab, e_, 256, AT[pr, 2048:2304], GT[pr, 2048:2304], akeys(ats, [4]), akeys(3, [4]), mode="act")
                            tl["done"] = done
                        attn_add(tl)
            attn_flush()
            if c == 3:
                chunks = []
                for cc in range(4):
                    S.dma(W[:, cc * 1024:(cc + 1) * 1024], wbf_d[l, :, cc * 5120 + 4096:cc * 5120 + 5120], writes=WK_ALL)
                    chunks.append((W[:, cc * 1024:(cc + 1) * 1024], WK_ALL, NA_AT[cc]))
                out_proj(l, b, qblks, chunks)

        def rope_evac(pa, dst, c0, n, dkeys, latent):
            rr = slice(64, 96)
            r2 = slice(96, 128)
            if latent:
                t1 = nxt("TMP", 3)
                S.op("dve", lambda e: e.tensor_tensor(out=TMP[t1][rr, :n], in0=PS[pa][rr, :n], in1=CS[rr, c0:c0 + n], op=ALU.mult),
                     reads=[("ps", pa), "CS"], writes=[("TMP", t1)])
                t2 = nxt("TMP", 3)
                S.op("dve", lambda e: e.tensor_tensor(out=TMP[t2][rr, :n], in0=PS[pa][r2, :n], in1=CS[r2, 2048 + c0:2048 + c0 + n], op=ALU.mult),
                     reads=[("ps", pa), "CS"], writes=[("TMP", t2)])
                S.op("pool", lambda e: e.tensor_tensor(out=dst[rr, c0:c0 + n], in0=TMP[t1][rr, :n], in1=TMP[t2][rr, :n], op=ALU.add),
                     reads=[("TMP", t1), ("TMP", t2)], writes=dkeys)
            else:
                S.op("dve", lambda e: e.tensor_copy(out=dst[rr, c0:c0 + n], in_=PS[pa][rr, :n]), reads=[("ps", pa)], writes=dkeys)

        def mla_shared(l, b):
            base = 20480
            S.dma(W[:, 0:3584], wbf_d[l, :, base:base + 3584], writes=WK_ALL)
            S.dma(WV[:], wbf_d[l, :, base + 3584:base + 4096], writes=["Wv"])
            W3 = W[:, 0:3584].rearrange("p (k f) -> p k f", f=448)
            CQN3 = slot(4, 2).rearrange("p (m t) -> p m t", t=T)
            CKVN = slot(6)
            for tb, c0, n in blocks5:
                pcq = [bank("g"), bank("g")]
                pkv = bank("g")
                pka = bank("a")
                for kc in range(8):
                    st, sp_ = (kc == 0), (kc == 7)
                    rd = WK_ALL + [("HT", tb)]
                    for m in range(2):
                        S.op("pe", lambda e: e.matmul(PS[pcq[m]][:, :n], lhsT=W3[:, kc, m * 128:(m + 1) * 128], rhs=HT3[:, kc, c0:c0 + n], start=st, stop=sp_),
                             reads=rd, writes=[("ps", pcq[m])])
                    S.op("pe", lambda e: e.matmul(PS[pkv][:, :n], lhsT=W3[:, kc, 256:384], rhs=HT3[:, kc, c0:c0 + n], start=st, stop=sp_),
                         reads=rd, writes=[("ps", pkv)])
                    S.op("pe", lambda e: e.matmul(PS[pka][64:128, :n], lhsT=W3[:, kc, 384:448], rhs=HT3[:, kc, c0:c0 + n], start=st, stop=sp_, tile_position=(0, 64)),
                         reads=rd, writes=[("ps", pka)])
                ssq = bank("s3")
                for m in range(2):
                    qi = nxt("SQ", 4)
                    S.op("act", lambda e: e.activation(out=SQ[qi][:, :n], in_=PS[pcq[m]][:, :n], func=AF.Square), reads=[("ps", pcq[m])], writes=[("SQ", qi)])
                    S.op("pe", lambda e: e.matmul(PS[ssq][:, :n], lhsT=ONES[:], rhs=SQ[qi][:, :n], start=(m == 0), stop=(m == 1)),
                         reads=[("SQ", qi), "ONES"], writes=[("ps", ssq)])
                ri = rstd_from(ssq, n, 1.0 / 256)
                for m in range(2):
                    S.op("dve", lambda e: e.scalar_tensor_tensor(out=CQN3[:, m, c0:c0 + n], in0=PS[pcq[m]][:, :n], scalar=SM[:, 112 + l * 2 + m:113 + l * 2 + m],
                                                                 in1=RS[ri][:, :n], op0=ALU.mult, op1=ALU.mult),
                         reads=[("ps", pcq[m]), ("RS", ri), "SM"], writes=[("A", 4 + m, tb)])
                ssk = bank("s3")
                qi = nxt("SQ", 4)
                S.op("act", lambda e: e.activation(out=SQ[qi][:, :n], in_=PS[pkv][:, :n], func=AF.Square), reads=[("ps", pkv)], writes=[("SQ", qi)])
                S.op("pe", lambda e: e.matmul(PS[ssk][:, :n], lhsT=ONES[:], rhs=SQ[qi][:, :n], start=True, stop=True),
                     reads=[("SQ", qi), "ONES"], writes=[("ps", ssk)])
                ri = rstd_from(ssk, n, 1.0 / 128)
                S.op("dve", lambda e: e.scalar_tensor_tensor(out=CKVN[:, c0:c0 + n], in0=PS[pkv][:, :n], scalar=SM[:, 116 + l:117 + l],
                                                             in1=RS[ri][:, :n], op0=ALU.mult, op1=ALU.mult),
                     reads=[("ps", pkv), ("RS", ri), "SM"], writes=[("A", 6, tb)])
                rope_evac(pka, slot(2), c0, n, [("A", 2, tb)], tb < 4)

        MLA_AT = [7, 8, 1, 9]

        def mla_pair(l, b, c):
            last = (l == NL - 1)
            qblks = blocks5[:4] if last else blocks5
            base = 24576 + c * 2688
            S.dma(W[:, 0:1664], wbf_d[l, :, base:base + 1664], writes=["W0"])
            WG3 = W[:, 0:1024].rearrange("p (k f) -> p k f", f=128)
            WQ3 = W[:, 1024:1536].rearrange("p (m f) -> p m f", f=256)
            WK = W[:, 1536:1664]
            CQN3 = slot(4, 2).rearrange("p (m t) -> p m t", t=T)
            CKVN = slot(6)
            ats = MLA_AT[c]
            QM, KM, GT, AT = slot(0), slot(2), slot(3), slot(ats)
            proj_fm((0, 128), 3, qblks, silu_evac(3), w3=WG3, wkey=["W0"])
            for tt in range(18):
                pb_ = bank("x")
                S.op("pe", lambda e: e.matmul(PS[pb_][:, 0:128], lhsT=CKVN[:, tt * 128:(tt + 1) * 128], rhs=WV[:, c * 128:(c + 1) * 128], start=True, stop=True),
                     reads=["Wv", ("A", 6, tt // 4)], writes=[("ps", pb_)])
                va_evac(pb_, tt)
            for e_ in range(2):
                nm = slice(64 * e_, 64 * e_ + 64)
                for tb, c0, n in qblks:
                    p1 = bank("x")
                    for m in range(2):
                        S.op("pe", lambda e: e.matmul(PS[p1][:, :n], lhsT=WQ3[:, m, e_ * 128:(e_ + 1) * 128], rhs=CQN3[:, m, c0:c0 + n], start=(m == 0), stop=(m == 1)),
                             reads=["W0", ("A", 4 + m, tb)], writes=[("ps", p1)])
                    if tb % 2 == 0:
                        S.op("act", lambda e: e.activation(out=QM[0:64, c0:c0 + n], in_=PS[p1][0:64, :n], func=AF.Identity), reads=[("ps", p1)], writes=[("A", 0, tb)])
                    else:
                        S.op("dve", lambda e: e.tensor_copy(out=QM[0:64, c0:c0 + n], in_=PS[p1][0:64, :n]), reads=[("ps", p1)], writes=[("A", 0, tb)])
                    rope_evac(p1, QM, c0, n, [("A", 0, tb)], tb < 4)
                for tb, c0, n in blocks5:
                    pk = bank("x")
                    S.op("pe", lambda e: e.matmul(PS[pk][0:64, :n], lhsT=WK[:, e_ * 64:(e_ + 1) * 64], rhs=CKVN[:, c0:c0 + n], start=True, stop=True),
                         reads=["W0", ("A", 6, tb)], writes=[("ps", pk)])
                    if tb % 2 == 1:
                        S.op("act", lambda e: e.activation(out=KM[0:64, c0:c0 + n], in_=PS[pk][0:64, :n], func=AF.Identity), reads=[("ps", pk)], writes=[("A", 2, tb)])
                    else:
                        S.op("dve", lambda e: e.tensor_copy(out=KM[0:64, c0:c0 + n], in_=PS[pk][0:64, :n]), reads=[("ps", pk)], writes=[("A", 2, tb)])
                if c == 3 and e_ == 1:
                    for cc in range(4):
                        bs = 24576 + cc * 2688 + 1664
                        S.dma(W[:, cc * 1024:(cc + 1) * 1024], wbf_d[l, :, bs:bs + 1024], writes=WK_ALL)
                for tb, c0, n in qblks:
                    ab = bank("a")
                    kcs = list(range(18)) if tb < 4 else [16, 17]
                    for i_, kc in enumerate(kcs):
                        first, lastt = (i_ == 0), (i_ == len(kcs) - 1)
                        tl = dict(qk=mk_qk(KM[0:96, kc * 128:(kc + 1) * 128], QM[0:96, c0:c0 + n], n, [("A", 2, kc // 4), ("A", 0, tb)]),
                                  n=n, scale=MLA_SCALE, post=None,
                                  pv=mk_pv(ab, 0, n, VA3[:, kc, 64 * e_:64 * e_ + 128], ("VA", kc), first, lastt))
                        if lastt:
                            def done(ab=ab, e_=e_, n=n, c0=c0, tb=tb, nm=nm):
                                normalize(ab, e_, n, AT[nm, c0:c0 + n], GT[nm, c0:c0 + n], [("A", ats, tb)], [("A", 3, tb)], mode="dve")
                            tl["done"] = done
                        attn_add(tl)
                attn_flush()
            if c == 3:
                chunks = [(W[:, cc * 1024:(cc + 1) * 1024], WK_ALL, MLA_AT[cc]) for cc in range(4)]
                out_proj(l, b, qblks, chunks)

        def final_out(b):
            for tb, c0, n in blocks5[:4]:
                ssb = sumsq_blocks(tb, c0, n)
                ri = rstd_from(ssb, n, 1.0 / 1024)
                for kc in range(8):
                    S.op("dve", lambda e: e.scalar_tensor_tensor(out=XT3[:, kc, c0:c0 + n], in0=XT3[:, kc, c0:c0 + n], scalar=SM[:, 104 + kc:105 + kc],
                                                                 in1=RS[ri][:, :n], op0=ALU.mult, op1=ALU.mult),
                         reads=[("XT", tb), ("RS", ri), "SM"], writes=[("XT", tb)])
                for t4 in range(4):
                    tt = tb * 4 + t4
                    xi = nxt("XS", 2)
                    for half in range(2):
                        pb_ = bank("x")
                        for k4 in range(4):
                            kc = half * 4 + k4
                            S.op("pe", lambda e: e.transpose(out=PS[pb_][:, k4 * 128:(k4 + 1) * 128], in_=XT3[:, kc, tt * 128:(tt + 1) * 128], identity=IDT[:]),
                                 reads=[("XT", tb), "IDT"], writes=[("ps", pb_)])
                        if half == 0:
                            S.op("dve", lambda e: e.tensor_copy(out=XS[xi][:, 0:512], in_=PS[pb_][:, :]), reads=[("ps", pb_)], writes=XSK[xi])
                        else:
                            S.op("act", lambda e: e.activation(out=XS[xi][:, 512:1024], in_=PS[pb_][:, :], func=AF.Identity), reads=[("ps", pb_)], writes=XSK[xi])
                    S.dma(out_d[b, tt * 128:(tt + 1) * 128, :], XS[xi], reads=XSK[xi])

        for b in range(NB):
            load_x(b)
            for l in range(NL):
                norm_mod(l, b)
                for c in range(4):
                    na_pair(l, b, c)
                mla_shared(l, b)
                for c in range(4):
                    mla_pair(l, b, c)
            final_out(b)
        S.finish()
    return nc


def _prep_shared(inp):
    w_in, w_out, w_uq, w_ukv = (np.asarray(inp[k], np.float32) for k in ("w_in", "w_out", "w_uq", "w_ukv"))
    blob = np.zeros((2, 128, NCOL), np.float32)
    sw = np.arange(32) ^ 1
    for l in range(2):
        wi = w_in[l].reshape(8, 128, 2976)
        wo = w_out[l].reshape(8, 128, 1024)
        for c in range(4):
            base = c * 5120
            sel = np.stack([wi[:, :, s * 512 + c * 128: s * 512 + (c + 1) * 128] for s in range(4)], axis=2)
            blob[l, :, base:base + 4096] = sel.transpose(1, 0, 2, 3).reshape(128, 4096)
            blob[l, :, base + 4096:base + 5120] = wo[c]
        base = 20480
        cols = np.concatenate([np.arange(2048, 2432), 2432 + np.arange(32), 2432 + sw])
        blob[l, :, base:base + 3584] = wi[:, :, cols].transpose(1, 0, 2).reshape(128, 3584)
        vcols = (np.arange(8)[:, None] * 128 + 64 + np.arange(64)[None]).reshape(-1)
        blob[l, :, base + 3584:base + 4096] = w_ukv[l][:, vcols]
        wq = w_uq[l].reshape(2, 128, 768)
        for c in range(4):
            base = 24576 + c * 2688
            blob[l, :, base:base + 1024] = wi[:, :, 2464 + c * 128:2464 + (c + 1) * 128].transpose(1, 0, 2).reshape(128, 1024)
            qc = []
            for e in range(2):
                h = 2 * c + e
                qc += [h * 96 + np.arange(64), h * 96 + 64 + np.arange(32), h * 96 + 64 + sw]
            qc = np.concatenate(qc)
            blob[l, :, base + 1024:base + 1536] = wq[:, :, qc].transpose(1, 0, 2).reshape(128, 512)
            kc_ = np.concatenate([(2 * c + e) * 128 + np.arange(64) for e in range(2)])
            blob[l, :, base + 1536:base + 1664] = w_ukv[l][:, kc_]
            blob[l, :, base + 1664:base + 2688] = wo[4 + c]
    w_ada = np.asarray(inp["w_ada"], np.float32)
    wada = np.stack([w_ada[l].reshape(8, 128, 24, 128).transpose(2, 1, 0, 3).reshape(24, 128, 1024) for l in range(2)]).reshape(48, 128, 1024)
    t = np.arange(2048)
    row = (t // 64).astype(np.float32)
    col = (t % 64).astype(np.float32)
    inv = (1.0 / (np.float32(10000.0) ** (np.arange(0, 16, 2, dtype=np.float32) / np.float32(16)))).astype(np.float32)
    ang = np.concatenate([row[:, None] * inv[None], col[:, None] * inv[None]], axis=-1).astype(np.float32)
    f = np.arange(32)
    cosT = np.cos(ang.astype(np.float64))[:, f // 2].T
    sinT = np.sin(ang.astype(np.float64))[:, f // 2].T * np.where(f % 2 == 0, -1.0, 1.0)[:, None]
    cs = np.zeros((2, 128, 2048), np.float32)
    cs[0, 64:96] = cosT
    cs[1, 64:96] = sinT
    cs[1, 96:128] = sinT
    rpb = np.asarray(inp["na_rpb"], np.float32)
    p = np.arange(128)
    kr = (p // 64)[:, None, None, None, None]
    kcc = (p % 64)[:, None, None, None, None]
    v = np.arange(2)[None, :, None, None, None]
    j = np.arange(4)[None, None, :, None, None]
    d = np.arange(16)[None, None, None, :, None]
    qc_ = np.arange(16)[None, None, None, None, :]
    dr = 7 + kr - d
    q = 16 * j + qc_
    q_cs = np.clip(q - 8, 0, 48)
    valid = (np.abs(dr) <= 7) & ((v == 1) | ((dr >= -4) & (dr <= 3))) & (kcc >= q_cs) & (kcc < q_cs + 16)
    valid = np.broadcast_to(valid, (128, 2, 4, 16, 16))
    ir = np.broadcast_to(np.clip(dr + 7, 0, 14), (128, 2, 4, 16, 16))
    ic = np.broadcast_to(np.clip(kcc - q + 15, 0, 30), (128, 2, 4, 16, 16))
    g = rpb[:, :, ir, ic]
    nab = np.where(valid[None, None], g, np.float32(-30000.0)).astype(np.float32).reshape(2, 8, 128, 2048)
    return dict(wblob=blob, wada=np.ascontiguousarray(wada), ident=np.eye(128, dtype=np.float32), cs=cs, nab=np.ascontiguousarray(nab))


def _prep_small(inp, bidx):
    c = np.asarray(inp["c"], np.float32)
    cc = np.concatenate([c[bidx], np.asarray(inp["c_ctx"], np.float32)[None]], axis=0)
    if cc.shape[0] < 5:
        cc = np.concatenate([cc[:-1], np.zeros((5 - cc.shape[0], 1024), np.float32), cc[-1:]], axis=0)
    sm = np.zeros((128, NSMALL), np.float32)
    sm[:, 0:40] = cc.reshape(5, 8, 128).transpose(2, 1, 0).reshape(128, 40)
    sm[:, 40:88] = np.asarray(inp["b_ada"], np.float32).reshape(2, 24, 128).transpose(2, 0, 1).reshape(128, 48)
    sm[:, 88:104] = np.asarray(inp["norm_g"], np.float32).reshape(2, 8, 128).transpose(2, 0, 1).reshape(128, 16)
    sm[:, 104:112] = np.asarray(inp["final_norm_g"], np.float32).reshape(8, 128).T
    sm[:, 112:116] = np.asarray(inp["q_norm_g"], np.float32).reshape(2, 2, 128).transpose(2, 0, 1).reshape(128, 4)
    sm[:, 116:118] = np.asarray(inp["kv_norm_g"], np.float32).T
    return sm


_NC_CACHE = {}


def _get_nc(nb):
    if nb not in _NC_CACHE:
        _NC_CACHE[nb] = build(nb)
    return _NC_CACHE[nb]


def run_cores(inp, batch_lists):
    nb = len(batch_lists[0])
    shared = _prep_shared(inp)
    x = np.asarray(inp["x"], np.float32)
    ctx = np.asarray(inp["ctx"], np.float32)
    in_maps = []
    for bl in batch_lists:
        m = dict(shared)
        m["x"] = np.ascontiguousarray(x[bl])
        m["ctx"] = np.ascontiguousarray(ctx[bl])
        m["small"] = _prep_small(inp, bl)
        in_maps.append(m)
    nc = _get_nc(nb)
    res = run_bass_kernel_spmd(nc, in_maps, core_ids=list(range(len(batch_lists))))
    return [r["out"] for r in res.results]


def kernel(**inputs):
    B = inputs["x"].shape[0]
    per = B // N_CORES
    bls = [list(range(i * per, (i + 1) * per)) for i in range(N_CORES)]
    outs = run_cores(inputs, bls)
    return np.concatenate(outs, axis=0).astype(np.float32)
```
